# Optimizing a Trainium2 kernel written in Bass

```python
import jax, jax.numpy as jnp
from jax import lax
import numpy as np

D_MODEL = 2048
BATCH = 4
SEQ = 8192
DEPTH = 1

PLE_DIM = 256
MIX_WIDTH = D_MODEL
LRU_WIDTH = MIX_WIDTH // 2
LRU_BLOCKS = 16
LRU_BLOCK_DIM = LRU_WIDTH // LRU_BLOCKS
CONV_WIDTH = 4
LRU_C = 8.0
MLA_HEADS = 8
QK_NOPE_DIM = 128
QK_ROPE_DIM = 64
V_HEAD_DIM = 128
Q_LORA_RANK = 512
KV_LORA_RANK = 512
ROPE_THETA = 10000.0
Q_BLOCK = 128
D_FF = 5632
LN_EPS = 1e-5
RMS_EPS = 1e-6
DEEPNORM_ALPHA = (2 * DEPTH) ** 0.25
DEEPNORM_BETA = (8 * DEPTH) ** -0.25
IN_SPLITS = (LRU_WIDTH, 2 * LRU_WIDTH, 2 * LRU_WIDTH + Q_LORA_RANK,
             2 * LRU_WIDTH + Q_LORA_RANK + KV_LORA_RANK)
IN_PROJ_DIM = 2 * LRU_WIDTH + Q_LORA_RANK + KV_LORA_RANK + QK_ROPE_DIM

kernel_name = "hymba_rglru_mla_macaron_deepnorm"


def layer_norm(x, g, b):
    xf = x.astype(jnp.float32)
    mu = jnp.mean(xf, axis=-1, keepdims=True)
    xc = xf - mu
    var = jnp.mean(xc * xc, axis=-1, keepdims=True)
    y = xc * lax.rsqrt(var + LN_EPS) * g.astype(jnp.float32) + b.astype(jnp.float32)
    return y.astype(x.dtype)


def rms_norm(x, g):
    xf = x.astype(jnp.float32)
    y = xf * lax.rsqrt(jnp.mean(xf * xf, axis=-1, keepdims=True) + RMS_EPS) * g.astype(jnp.float32)
    return y.astype(x.dtype)


def swiglu(x, w_gate, w_up, w_down):
    return (jax.nn.silu(x @ w_gate) * (x @ w_up)) @ w_down


def rope_tables(positions):
    inv_freq = ROPE_THETA ** (-jnp.arange(0, QK_ROPE_DIM, 2, dtype=jnp.float32) / QK_ROPE_DIM)
    ang = positions.astype(jnp.float32)[..., None] * inv_freq
    return jnp.cos(ang), jnp.sin(ang)


def apply_rope(x, cos, sin):
    xf = x.astype(jnp.float32)
    x1, x2 = jnp.split(xf, 2, axis=-1)
    y = jnp.concatenate([x1 * cos - x2 * sin, x2 * cos + x1 * sin], axis=-1)
    return y.astype(x.dtype)


def causal_dwconv(x, w, b):
    s = x.shape[1]
    xp = jnp.pad(x, ((0, 0), (CONV_WIDTH - 1, 0), (0, 0)))
    return b + sum(xp[:, k:k + s] * w[k] for k in range(CONV_WIDTH))


def rg_lru(x, w_a, b_a, w_x, b_x, lam):
    bsz, s, _ = x.shape
    xb = x.reshape(bsz, s, LRU_BLOCKS, LRU_BLOCK_DIM)
    rec_gate = jax.nn.sigmoid(jnp.einsum('bsgi,gij->bsgj', xb, w_a) + b_a).reshape(bsz, s, LRU_WIDTH)
    in_gate = jax.nn.sigmoid(jnp.einsum('bsgi,gij->bsgj', xb, w_x) + b_x).reshape(bsz, s, LRU_WIDTH)
    log_a = -LRU_C * rec_gate.astype(jnp.float32) * jax.nn.softplus(-lam.astype(jnp.float32))
    a = jnp.exp(log_a)
    u = jnp.sqrt(-jnp.expm1(2.0 * log_a)) * (in_gate * x).astype(jnp.float32)

    def combine(left, right):
        a_l, h_l = left
        a_r, h_r = right
        return a_l * a_r, a_r * h_l + h_r

    _, h = lax.associative_scan(combine, (a, u), axis=1)
    return h.astype(x.dtype)


def mla_attention(c_q, c_kv, k_pe_raw, cos, sin, q_norm_g, w_q_up, kv_norm_g, w_kv_up):
    bsz, s, _ = c_q.shape
    q = (rms_norm(c_q, q_norm_g) @ w_q_up).reshape(bsz, s, MLA_HEADS, QK_NOPE_DIM + QK_ROPE_DIM)
    q_nope, q_pe = q[..., :QK_NOPE_DIM], q[..., QK_NOPE_DIM:]
    q_pe = apply_rope(q_pe, cos[:, :, None, :], sin[:, :, None, :])
    k_pe = apply_rope(k_pe_raw, cos, sin)
    kv = (rms_norm(c_kv, kv_norm_g) @ w_kv_up).reshape(bsz, s, MLA_HEADS, QK_NOPE_DIM + V_HEAD_DIM)
    k_nope, v = kv[..., :QK_NOPE_DIM], kv[..., QK_NOPE_DIM:]
    scale = (QK_NOPE_DIM + QK_ROPE_DIM) ** -0.5
    outs = []
    for blk in range(s // Q_BLOCK):
        q0 = blk * Q_BLOCK
        q1 = q0 + Q_BLOCK
        sc = (jnp.einsum('bqhd,bkhd->bhqk', q_nope[:, q0:q1], k_nope[:, :q1])
              + jnp.einsum('bqhr,bkr->bhqk', q_pe[:, q0:q1], k_pe[:, :q1]))
        sc = sc.astype(jnp.float32) * scale
        mask = jnp.arange(q1)[None, :] <= (q0 + jnp.arange(Q_BLOCK))[:, None]
        prob = jax.nn.softmax(jnp.where(mask, sc, -jnp.inf), axis=-1).astype(v.dtype)
        outs.append(jnp.einsum('bhqk,bkhd->bqhd', prob, v[:, :q1]))
    o = jnp.concatenate(outs, axis=1)
    return o.reshape(bsz, s, MLA_HEADS * V_HEAD_DIM)


def setup_inputs(seed: int = 0) -> dict:
    key = jax.random.key(seed)
    ks = iter(jax.random.split(key, 40))

    def nrm(shape, scale):
        return jax.random.normal(next(ks), shape, jnp.float32) * scale

    def gain(shape):
        return 1.0 + nrm(shape, 0.01)

    L = DEPTH
    u = jax.random.uniform(next(ks), (L, LRU_WIDTH), jnp.float32, minval=0.9, maxval=0.999)
    a0 = u ** (1.0 / LRU_C)
    lru_lambda = jnp.log(a0) - jnp.log1p(-a0)
    return {
        "x": nrm((BATCH, SEQ, D_MODEL), 1.0),
        "p": nrm((DEPTH, BATCH, SEQ, PLE_DIM), 1.0),
        "positions": jnp.broadcast_to(jnp.arange(SEQ, dtype=jnp.int32), (BATCH, SEQ)),
        "ffn1_w_gate": nrm((L, D_MODEL, D_FF), D_MODEL ** -0.5),
        "ffn1_w_up": nrm((L, D_MODEL, D_FF), D_MODEL ** -0.5),
        "ffn1_w_down": nrm((L, D_FF, D_MODEL), DEEPNORM_BETA * D_FF ** -0.5),
        "ln1_g": gain((L, D_MODEL)),
        "ln1_b": nrm((L, D_MODEL), 0.01),
        "w_in": nrm((L, D_MODEL, IN_PROJ_DIM), D_MODEL ** -0.5),
        "conv_w": nrm((L, CONV_WIDTH, LRU_WIDTH), CONV_WIDTH ** -0.5),
        "conv_b": nrm((L, LRU_WIDTH), 0.01),
        "lru_w_a": nrm((L, LRU_BLOCKS, LRU_BLOCK_DIM, LRU_BLOCK_DIM), LRU_BLOCK_DIM ** -0.5),
        "lru_b_a": nrm((L, LRU_BLOCKS, LRU_BLOCK_DIM), 0.01),
        "lru_w_x": nrm((L, LRU_BLOCKS, LRU_BLOCK_DIM, LRU_BLOCK_DIM), LRU_BLOCK_DIM ** -0.5),
        "lru_b_x": nrm((L, LRU_BLOCKS, LRU_BLOCK_DIM), 0.01),
        "lru_lambda": lru_lambda,
        "q_norm_g": gain((L, Q_LORA_RANK)),
        "w_q_up": nrm((L, Q_LORA_RANK, MLA_HEADS * (QK_NOPE_DIM + QK_ROPE_DIM)), Q_LORA_RANK ** -0.5),
        "kv_norm_g": gain((L, KV_LORA_RANK)),
        "w_kv_up": nrm((L, KV_LORA_RANK, MLA_HEADS * (QK_NOPE_DIM + V_HEAD_DIM)), KV_LORA_RANK ** -0.5),
        "w_out": nrm((L, MIX_WIDTH, D_MODEL), DEEPNORM_BETA * MIX_WIDTH ** -0.5),
        "ln2_g": gain((L, D_MODEL)),
        "ln2_b": nrm((L, D_MODEL), 0.01),
        "ffn2_w_gate": nrm((L, D_MODEL, D_FF), D_MODEL ** -0.5),
        "ffn2_w_up": nrm((L, D_MODEL, D_FF), D_MODEL ** -0.5),
        "ffn2_w_down": nrm((L, D_FF, D_MODEL), DEEPNORM_BETA * D_FF ** -0.5),
        "ln3_g": gain((L, D_MODEL)),
        "ln3_b": nrm((L, D_MODEL), 0.01),
        "ple_w_gate": nrm((L, D_MODEL, D_MODEL), D_MODEL ** -0.5),
        "ple_b_gate": nrm((L, D_MODEL), 0.01),
        "ple_w_proj": nrm((L, PLE_DIM, D_MODEL), DEEPNORM_BETA * PLE_DIM ** -0.5),
        "ln4_g": gain((L, D_MODEL)),
        "ln4_b": nrm((L, D_MODEL), 0.01),
    }


def reference(x, p, positions, ffn1_w_gate, ffn1_w_up, ffn1_w_down, ln1_g, ln1_b, w_in,
              conv_w, conv_b, lru_w_a, lru_b_a, lru_w_x, lru_b_x, lru_lambda, q_norm_g, w_q_up,
              kv_norm_g, w_kv_up, w_out, ln2_g, ln2_b, ffn2_w_gate, ffn2_w_up, ffn2_w_down,
              ln3_g, ln3_b, ple_w_gate, ple_b_gate, ple_w_proj, ln4_g, ln4_b):
    cos, sin = rope_tables(positions)
    for i in range(DEPTH):
        x = layer_norm(DEEPNORM_ALPHA * x + 0.5 * swiglu(x, ffn1_w_gate[i], ffn1_w_up[i], ffn1_w_down[i]),
                       ln1_g[i], ln1_b[i])
        z = x @ w_in[i]
        lru_in, lru_gate, c_q, c_kv, k_pe_raw = jnp.split(z, IN_SPLITS, axis=-1)
        h = rg_lru(causal_dwconv(lru_in, conv_w[i], conv_b[i]),
                   lru_w_a[i], lru_b_a[i], lru_w_x[i], lru_b_x[i], lru_lambda[i])
        y_lru = jax.nn.gelu(lru_gate) * h
        y_mla = mla_attention(c_q, c_kv, k_pe_raw, cos, sin,
                              q_norm_g[i], w_q_up[i], kv_norm_g[i], w_kv_up[i])
        mix = jnp.concatenate([y_lru, y_mla], axis=-1) @ w_out[i]
        x = layer_norm(DEEPNORM_ALPHA * x + mix, ln2_g[i], ln2_b[i])
        x = layer_norm(DEEPNORM_ALPHA * x + 0.5 * swiglu(x, ffn2_w_gate[i], ffn2_w_up[i], ffn2_w_down[i]),
                       ln3_g[i], ln3_b[i])
        ple = jax.nn.sigmoid(x @ ple_w_gate[i] + ple_b_gate[i]) * (p[i].astype(x.dtype) @ ple_w_proj[i])
        x = layer_norm(DEEPNORM_ALPHA * x + ple, ln4_g[i], ln4_b[i])
    return x
```

```python
import numpy as np
import concourse.bass as bass
import concourse.mybir as mybir
from concourse.bass_utils import run_bass_kernel_spmd

F32 = mybir.dt.float32
BF16 = mybir.dt.bfloat16
I32 = mybir.dt.int32
AF = mybir.ActivationFunctionType
ALU = mybir.AluOpType
AX = mybir.AxisListType

T = 512
D = 2048
KC = 16
DFF = 5632
NFG = 11
ALPHA = 2.0 ** 0.25
LN_EPS = 1e-5
RMS_EPS = 1e-6
QSCALE = 192.0 ** -0.5
NSLOT = 5
LSZ = 528
NL = 14
NEG = -30000.0
PI = float(np.pi)

C_CV = 0
C_QG = 64
C_KVG = 68
C_FLAG = 72
C_ROPE = 74
C_ID = 76
C_CM = 204
NCST = 716


class _Op:
    __slots__ = ("eng", "fn", "deps", "signal", "sigval", "dma", "ninc")


class Sched:
    def __init__(self):
        self.ops = []
        self.lastw = {}
        self.readers = {}

    def add(self, eng, fn, reads=(), writes=(), dma=None, ninc=1):
        idx = len(self.ops)
        deps = {}

        def dep(j):
            o = self.ops[j]
            if o.dma is None and o.eng == "pe" and eng == "pe" and dma is None:
                return
            k = ("d", o.dma) if o.dma is not None else ("e", o.eng)
            if k not in deps or deps[k] < j:
                deps[k] = j

        for k in reads:
            w = self.lastw.get(k)
            if w is not None:
                dep(w)
        for k in writes:
            w = self.lastw.get(k)
            if w is not None:
                dep(w)
            for r in self.readers.get(k, ()):
                dep(r)
        for k in writes:
            self.lastw[k] = idx
            self.readers[k] = []
        for k in reads:
            if k not in writes:
                self.readers.setdefault(k, []).append(idx)
        op = _Op()
        op.eng, op.fn, op.dma, op.ninc = eng, fn, dma, ninc
        op.deps = list(deps.values())
        op.signal = False
        op.sigval = 0
        for j in op.deps:
            self.ops[j].signal = True
        self.ops.append(op)
        return idx

    def emit(self, nc, stack):
        engs = ("pe", "act", "dve", "pool", "sp")
        esem = {e: stack.enter_context(nc.semaphore("s_" + e)) for e in engs}
        dsem = {}
        ecnt = {e: 0 for e in engs}
        dcnt = {}
        for op in self.ops:
            if op.dma is not None:
                if op.dma not in dsem:
                    dsem[op.dma] = stack.enter_context(nc.semaphore("d_" + op.dma))
                    dcnt[op.dma] = 0
                dcnt[op.dma] += 16 * op.ninc
                op.sigval = dcnt[op.dma]
            elif op.signal:
                ecnt[op.eng] += 1
                op.sigval = ecnt[op.eng]
        ops = self.ops
        block = stack.enter_context(nc.Block())

        def run(e, name):
            waited = {}
            for op in ops:
                if op.eng != name:
                    continue
                for j in op.deps:
                    p = ops[j]
                    sem = dsem[p.dma] if p.dma is not None else esem[p.eng]
                    key = id(sem)
                    if waited.get(key, 0) < p.sigval:
                        e.wait_ge(sem, p.sigval)
                        waited[key] = p.sigval
                if op.fn is None:
                    continue
                r = op.fn(e)
                if op.dma is not None:
                    for inst in r:
                        inst.then_inc(dsem[op.dma], 16)
                elif op.signal:
                    inst = r[-1] if isinstance(r, (list, tuple)) else r
                    inst.then_inc(esem[name], 1)

        @block.tensor
        def _(e):
            run(e, "pe")

        @block.scalar
        def _(e):
            run(e, "act")

        @block.vector
        def _(e):
            run(e, "dve")

        @block.gpsimd
        def _(e):
            run(e, "pool")

        @block.sync
        def _(e):
            run(e, "sp")


def build(nth, stop=None):
    import contextlib

    nc = bass.Bass("TRN2", target_bir_lowering=False)
    S2 = nth * T
    NTOK = 2 * S2
    NTILE = 2 * nth
    NBLK = NTOK // 128

    def din(name, shape, dt=F32):
        return nc.dram_tensor(name, shape, dt, kind="ExternalInput").ap()

    def dint(name, shape, dt=BF16):
        return nc.dram_tensor(name, shape, dt, kind="Internal").ap()

    xs = din("xs", [NTOK, D])
    ps_in = din("ps", [S2, 256])
    pos = din("pos", [1, NTOK], I32)
    cst_in = din("cst", [128, NCST])
    wbd_in = din("wbd", [128, 2048])
    lnp = din("lnp", [4, 4096])
    bg_in = din("bg", [1, 2048])
    wshapes = dict(w1g=[D, DFF], w1u=[D, DFF], w1d=[DFF, D], win=[D, 3200], wkv=[512, 2048],
                   wq=[512, 2048], wout=[D, D], w2g=[D, DFF], w2u=[D, DFF], w2d=[DFF, D],
                   wpg=[D, D], wpp=[256, D])
    w32 = {k: din(k, v) for k, v in wshapes.items()}
    wb = {k: dint(k + "_b", v) for k, v in wshapes.items()}
    KTs = dint("KTs", [8, 128, NTOK])
    KPEs = dint("KPEs", [64, NTOK])
    Vs = dint("Vs", [8, 128, NBLK, 128])
    yout = nc.dram_tensor("y", [S2, D], F32, kind="ExternalOutput").ap()

    sc = Sched()
    stack = contextlib.ExitStack()
    with stack:
        def sb(name, shape, dt):
            return stack.enter_context(nc.sbuf_tensor(name, shape, dt))

        xtm = sb("xtm", [128, 4, D], F32)
        xT = sb("xT", [128, KC, T], BF16)
        r2 = sb("r2", [128, 32, T], BF16)
        wr = [sb(f"wr{i}", [128, 8192], BF16) for i in range(NSLOT)]
        Lb = sb("Lb", [128, NL, LSZ], F32)
        cst = sb("cst_sb", [128, NCST], F32)
        wabd = sb("wabd", [128, 8, 128], BF16)
        wxbd = sb("wxbd", [128, 8, 128], BF16)
        identb = sb("identb", [128, 128], BF16)
        onesb = sb("onesb", [128, 128], BF16)
        clt = sb("clt", [128, 3, 8], F32)
        halo = sb("halo", [128, 8, 4], F32)
        carry = sb("carry", [128, 8], F32)
        sm = sb("sm", [128, 64], F32)
        bnst = sb("bnst", [128, 4, 6], F32)
        mxt = sb("mxt", [128, 40], F32)
        ps = [stack.enter_context(nc.psum_tensor(f"ps{i}", [128, 512], F32)) for i in range(8)]

        ident32 = cst[:, C_ID:C_ID + 128]
        cm3 = cst[:, C_CM:C_CM + 512]
        cv = cst[:, C_CV:C_CV + 64].rearrange("p (c j) -> p c j", j=8)
        flag_ap = cst[:, C_FLAG:C_FLAG + 1]
        mbias_ap = cst[:, C_FLAG + 1:C_FLAG + 2]
        invf_ap = cst[0:64, C_ROPE:C_ROPE + 1]
        sgn_ap = cst[0:64, C_ROPE + 1:C_ROPE + 2]

        def L(j, n=T, dt=F32, p=128):
            a = Lb[0:p, j, :]
            if dt == BF16:
                return a.bitcast(BF16)[:, 0:n]
            if dt == I32:
                return a.bitcast(I32)[:, 0:n]
            return a[:, 0:n]

        def kL(j):
            return ("L", j)

        bank_ctr = {}

        def bank(group):
            base = dict(g=0, u=2, d=4, m=6)[group]
            i = bank_ctr.get(group, 0)
            bank_ctr[group] = i + 1
            return base + (i % 2)

        def kps(b):
            return ("ps", b)

        xtm_keys = [("xtm", s, dc) for s in range(4) for dc in range(4)]
        xT_keys = [("xT", kc) for kc in range(KC)]

        def kr2(g):
            return ("r2", g)

        def mmg(out, pairs, reads, writes):
            def fn(e, out=out, pairs=pairs):
                n = len(pairs)
                inst = None
                for i, (l, r) in enumerate(pairs):
                    inst = e.matmul(out, l, r, start=(i == 0), stop=(i == n - 1))
                return inst
            sc.add("pe", fn, reads, writes)

        def act(out, in_, func, reads, writes, bias=None, scale=None, accum_out=None):
            def fn(e):
                kw = {}
                if bias is not None:
                    kw["bias"] = bias
                if scale is not None:
                    kw["scale"] = scale
                if accum_out is not None:
                    kw["accum_out"] = accum_out
                return e.activation(out, in_, func, **kw)
            sc.add("act", fn, reads, writes)

        def vop(eng, name, reads, writes, *a, **kw):
            def fn(e):
                return getattr(e, name)(*a, **kw)
            sc.add(eng, fn, reads, writes)

        ring_ctr = [0]

        def ring_load(dmas, reads):
            s = ring_ctr[0] % NSLOT
            ring_ctr[0] += 1
            slot = wr[s]

            def fn(e, dmas=dmas, slot=slot):
                return [e.dma_start(out=mk(slot), in_=src) for mk, src in dmas]
            sc.add("sp", fn, reads=reads, writes=[("w", s)], dma=f"w{s}", ninc=len(dmas))
            return s, slot

        def const_fn(e):
            return [e.dma_start(out=cst[:], in_=cst_in[:, :]),
                    e.dma_start(out=Lb[:, 0:4, 0:512], in_=wbd_in.rearrange("p (a b) -> p a b", b=512))]
        sc.add("sp", const_fn, reads=[], writes=["cst"] + [kL(j) for j in range(4)], dma="cst", ninc=2)

        cv_ctr = [0]

        def wkeys(name):
            return [("wb", name, r0) for r0 in range(0, wshapes[name][0], 256)]

        def conv_weight(name):
            src, dst = w32[name], wb[name]
            rows, cols = wshapes[name]
            step = 256
            for r0 in range(0, rows, step):
                r1 = min(rows, r0 + step)
                ch = cv_ctr[0] % 4
                cv_ctr[0] += 1

                def fn(e, src=src, dst=dst, r0=r0, r1=r1):
                    return [e.dma_start(out=dst[r0:r1, :], in_=src[r0:r1, :], max_dma_last_dim=2048)]
                sc.add("pool", fn, reads=[], writes=[("wb", name, r0), ("cvchain", ch)], dma=f"cv{ch}", ninc=1)

        for name in ("w1g", "w1u", "w1d", "win", "wkv", "wq", "wout", "w2g", "w2u", "w2d", "wpg", "wpp"):
            conv_weight(name)

        vop("dve", "memset", [], ["onesb"], onesb[:], 1.0)
        vop("dve", "tensor_copy", ["cst"], ["identb"], identb[:], ident32)
        vop("dve", "tensor_copy", [kL(0), kL(1)], ["wabd"], wabd[:].rearrange("p (a c) f -> p a (c f)", a=2),
            Lb[:, 0:2, 0:512])
        vop("dve", "tensor_copy", [kL(2), kL(3)], ["wxbd"], wxbd[:].rearrange("p (a c) f -> p a (c f)", a=2),
            Lb[:, 2:4, 0:512])
        vop("dve", "memset", [], ["halo"], halo[:], 0.0)
        vop("dve", "memset", [], ["carry"], carry[:], 0.0)
        act(clt[:, 0, :], cv[:, :, 7], AF.Exp, ["cst"], ["clt0"], scale=-1.0)
        act(clt[:, 0, :], clt[:, 0, :], AF.Ln, ["clt0"], ["clt0"], bias=1.0)
        vop("dve", "tensor_scalar", ["clt0"], ["cl"], clt[:, 1, :], clt[:, 0, :], -8.0, None, ALU.mult)
        vop("dve", "tensor_scalar", ["clt0"], ["cl"], clt[:, 2, :], clt[:, 0, :], -16.0, None, ALU.mult)

        def load_x(ti):
            src = xs[ti * T:(ti + 1) * T, :].rearrange("(s p) d -> p s d", p=128)

            def fn(e, src=src):
                return [e.dma_start(out=xtm[:], in_=src)]
            sc.add("sp", fn, reads=[], writes=xtm_keys, dma="x", ninc=1)

        evac_ctr = [0]

        def evac_copy(out, in_, reads, writes):
            evac_ctr[0] += 1
            if evac_ctr[0] % 2:
                act(out, in_, AF.Copy, reads, writes)
            else:
                vop("dve", "tensor_copy", reads, writes, out, in_)

        def transposes():
            for kc in range(KC):
                b = bank("m")
                pt = ps[b]

                def fn(e, kc=kc, pt=pt):
                    inst = None
                    for s in range(4):
                        inst = e.transpose(pt[:, s * 128:(s + 1) * 128], xtm[:, s, kc * 128:(kc + 1) * 128], ident32)
                    return inst
                sc.add("pe", fn, reads=[("xtm", s, kc // 4) for s in range(4)] + ["cst"], writes=[kps(b)])
                evac_copy(xT[:, kc, :], pt[:], [kps(b)], [("xT", kc)])

        def prescale():
            for s in range(4):
                act(xtm[:, s, :], xtm[:, s, :], AF.Copy, [("xtm", s, dc) for dc in range(4)], [("xtm", s, dc) for dc in range(4)], scale=ALPHA)

        def layernorm(n):
            s_, slot = ring_load([(lambda sl: sl[:].bitcast(F32)[:, 0:4096].rearrange("p (o n) -> p o n", o=1),
                                   lnp[n:n + 1, :].partition_broadcast(128))], reads=[])
            gb = slot[:].bitcast(F32)
            for s in range(4):
                keys = [("xtm", s, dc) for dc in range(4)]
                for dc in range(4):
                    vop("dve", "bn_stats", keys[dc:dc + 1], ["bnst"], bnst[:, dc, :], xtm[:, s, dc * 512:(dc + 1) * 512])
                vop("dve", "bn_aggr", ["bnst"], ["sm_ln"], sm[:, 0:2], bnst[:].rearrange("p a b -> p (a b)"))
                vop("dve", "tensor_scalar", ["sm_ln"], ["sm_ln2"], sm[:, 2:3], sm[:, 1:2], LN_EPS, None, ALU.add)
                act(sm[:, 3:4], sm[:, 2:3], AF.Sqrt, ["sm_ln2"], ["sm_ln3"])
                vop("dve", "reciprocal", ["sm_ln3"], ["sm_ln4"], sm[:, 4:5], sm[:, 3:4])
                vop("dve", "tensor_scalar", keys + ["sm_ln", "sm_ln4"], keys, xtm[:, s, :], xtm[:, s, :],
                    sm[:, 0:1], sm[:, 4:5], ALU.subtract, ALU.mult)
                vop("dve", "tensor_tensor", keys + [("w", s_)], keys, xtm[:, s, :], xtm[:, s, :], gb[:, 0:2048], ALU.mult)
                vop("dve", "tensor_tensor", keys + [("w", s_)], keys, xtm[:, s, :], xtm[:, s, :], gb[:, 2048:4096], ALU.add)

        def ffn(ng, nu, nd):
            wg, wu, wd = wb[ng], wb[nu], wb[nd]
            pend = {}

            def gu(fg):
                sA, wA = ring_load([(lambda sl: sl[:].rearrange("p (k f) -> p k f", f=512),
                                     wg[:, fg * 512:(fg + 1) * 512].rearrange("(k p) f -> p k f", p=128))],
                                   reads=wkeys(ng))
                sB, wB = ring_load([(lambda sl: sl[:].rearrange("p (k f) -> p k f", f=512),
                                     wu[:, fg * 512:(fg + 1) * 512].rearrange("(k p) f -> p k f", p=128))],
                                   reads=wkeys(nu))
                wAv = wA[:].rearrange("p (k f) -> p k f", f=512)
                wBv = wB[:].rearrange("p (k f) -> p k f", f=512)
                for j in range(4):
                    bgk, buk = bank("g"), bank("u")
                    mmg(ps[bgk][:], [(wAv[:, kc, j * 128:(j + 1) * 128], xT[:, kc, :]) for kc in range(KC)],
                        [("w", sA)] + xT_keys, [kps(bgk)])
                    mmg(ps[buk][:], [(wBv[:, kc, j * 128:(j + 1) * 128], xT[:, kc, :]) for kc in range(KC)],
                        [("w", sB)] + xT_keys, [kps(buk)])
                    lj = (fg * 4 + j) % 2
                    act(L(lj), ps[bgk][:], AF.Silu, [kps(bgk)], [kL(lj)])
                    g = (fg % 2) * 4 + j
                    vop("dve", "tensor_tensor", [kL(lj), kps(buk)], [kr2(g)], r2[:, g, :], L(lj), ps[buk][:], ALU.mult)

            def down(fg):
                sD, wD = ring_load([(lambda sl: sl[:].rearrange("p (j d) -> p j d", d=2048),
                                     wd[fg * 512:(fg + 1) * 512, :].rearrange("(j p) d -> p j d", p=128))],
                                   reads=wkeys(nd))
                wDv = wD[:].rearrange("p (j d) -> p j d", d=2048)
                gr = [(fg % 2) * 4 + j for j in range(4)]
                for s in range(4):
                    for dc in range(4):
                        b = bank("d")
                        mmg(ps[b][:], [(r2[:, gr[j], s * 128:(s + 1) * 128], wDv[:, j, dc * 512:(dc + 1) * 512])
                                       for j in range(4)],
                            [("w", sD)] + [kr2(g) for g in gr], [kps(b)])
                        vop("dve", "scalar_tensor_tensor", [kps(b), ("xtm", s, dc)], [("xtm", s, dc)],
                            xtm[:, s, dc * 512:(dc + 1) * 512], ps[b][:], 0.5,
                            xtm[:, s, dc * 512:(dc + 1) * 512], ALU.mult, ALU.add)

            gu(0)
            for fg in range(NFG):
                if fg + 1 < NFG:
                    gu(fg + 1)
                down(fg)

        def rms_feat(c0, gcol, src_slot_key, wslot_view, outL):
            bss = bank("m")
            for c in range(4):
                b = bank("g")
                mmg(ps[b][:], [(wslot_view[:, kc, c * 128:(c + 1) * 128], xT[:, kc, :]) for kc in range(KC)],
                    [src_slot_key] + xT_keys, [kps(b)])
                vop("dve", "tensor_copy", [kps(b)], [kL(4 + c)], L(4 + c), ps[b][:])
                if stop == "kv3a1":
                    continue
                vop("dve", "tensor_tensor", [kL(4 + c)], [kL(8)], L(8), L(4 + c), L(4 + c), ALU.mult)
                if stop == "kv3a2":
                    continue
                hi = Lb[:, 9, :].bitcast(BF16)[:, 0:512]
                lo = Lb[:, 9, :].bitcast(BF16)[:, 512:1024]
                vop("dve", "tensor_copy", [kL(8)], [kL(9)], hi, L(8))
                vop("dve", "tensor_tensor", [kL(8), kL(9)], [kL(9)], lo, L(8), hi, ALU.subtract)
                if stop == "kv3a3":
                    continue

                def fn(e, c=c, bss=bss, hi=hi, lo=lo):
                    e.matmul(ps[bss][:], onesb[:], hi, start=(c == 0), stop=False)
                    return e.matmul(ps[bss][:], onesb[:], lo, start=False, stop=(c == 3))
                sc.add("pe", fn, reads=["onesb", kL(9)], writes=[kps(bss)])
            if stop in ("kv3a1", "kv3a2", "kv3a3", "kv3a4"):
                return
            vop("dve", "tensor_scalar", [kps(bss)], [kL(10)], L(10), ps[bss][:], 1.0 / 512.0, RMS_EPS, ALU.mult, ALU.add)
            act(L(10), L(10), AF.Sqrt, [kL(10)], [kL(10)])
            vop("dve", "reciprocal", [kL(10)], [kL(11)], L(11), L(10))
            for c in range(4):
                vop("dve", "scalar_tensor_tensor", [kL(4 + c), kL(11), "cst"], [kL(outL + c // 2)],
                    Lb[:, outL + c // 2, :].bitcast(BF16)[:, (c % 2) * 512:(c % 2) * 512 + 512], L(4 + c),
                    cst[:, gcol + c:gcol + c + 1], L(11), ALU.mult, ALU.mult)

        def nrm(outL, c):
            return Lb[:, outL + c // 2, :].bitcast(BF16)[:, (c % 2) * 512:(c % 2) * 512 + 512]

        def rope_tables(ti):
            def fn(e):
                return [e.dma_start(out=L(2, T, I32, 64).rearrange("p (o n) -> p o n", o=1),
                                    in_=pos[0:1, ti * T:(ti + 1) * T].partition_broadcast(64))]
            sc.add("sp", fn, reads=[], writes=[kL(2)], dma="pos", ninc=1)
            A = lambda j: L(j, T, F32, 64)
            vop("dve", "tensor_copy", [kL(2)], [kL(3)], A(3), L(2, T, I32, 64))
            vop("dve", "tensor_scalar", [kL(3), "cst"], [kL(3)], A(3), A(3), invf_ap, None, ALU.mult)
            vop("dve", "tensor_scalar", [kL(3)], [kL(0)], A(0), A(3), 1.0 / (2 * PI), None, ALU.mult)
            vop("dve", "tensor_copy", [kL(0)], [kL(2)], L(2, T, I32, 64), A(0))
            vop("dve", "tensor_copy", [kL(2)], [kL(0)], A(0), L(2, T, I32, 64))
            C1 = 6.28125
            C2_ = float(2 * np.pi - 6.28125)
            vop("dve", "scalar_tensor_tensor", [kL(0), kL(3)], [kL(3)], A(3), A(0), -C1, A(3), ALU.mult, ALU.add)
            vop("dve", "scalar_tensor_tensor", [kL(0), kL(3)], [kL(3)], A(3), A(0), -C2_, A(3), ALU.mult, ALU.add)
            vop("dve", "tensor_scalar", [kL(3)], [kL(0)], A(0), A(3), PI, -2 * PI, ALU.is_gt, ALU.mult)
            vop("dve", "tensor_tensor", [kL(0), kL(3)], [kL(3)], A(3), A(3), A(0), ALU.add)
            vop("dve", "tensor_scalar", [kL(3)], [kL(0)], A(0), A(3), -PI, 2 * PI, ALU.is_lt, ALU.mult)
            vop("dve", "tensor_tensor", [kL(0), kL(3)], [kL(3)], A(3), A(3), A(0), ALU.add)
            vop("dve", "tensor_scalar", [kL(3)], [kL(3)], A(3), A(3), PI, -PI, ALU.min, ALU.max)
            act(A(1), A(3), AF.Sin, [kL(3)], [kL(1)])
            vop("dve", "tensor_scalar", [kL(1), "cst"], [kL(1)], A(1), A(1), sgn_ap, None, ALU.mult)
            vop("dve", "scalar_tensor_tensor", [kL(3)], [kL(3)], A(3), A(3), -1.0, A(3), ALU.mult, ALU.max)
            vop("dve", "tensor_scalar", [kL(3)], [kL(3)], A(3), A(3), -1.0, PI / 2, ALU.mult, ALU.add)
            act(A(0), A(3), AF.Sin, [kL(3)], [kL(0)])

        def rope_apply(pa, pb, out_ap, scale, reads, writes):
            vop("dve", "tensor_tensor", [kps(pa), kL(0)], [kL(12)], L(12, T, F32, 64), ps[pa][0:64, :], L(0, T, F32, 64), ALU.mult)
            vop("dve", "tensor_tensor", [kps(pb), kL(1)], [kL(13)], L(13, T, F32, 64), ps[pb][0:64, :], L(1, T, F32, 64), ALU.mult)
            vop("dve", "scalar_tensor_tensor", [kL(12), kL(13)] + reads, writes, out_ap, L(12, T, F32, 64), scale,
                L(13, T, F32, 64), ALU.mult, ALU.add)

        def kv_path(ti):
            sKV, wKV = ring_load([(lambda sl: sl[:].rearrange("p (k f) -> p k f", f=512),
                                   wb["win"][:, 2560:3072].rearrange("(k p) f -> p k f", p=128))], reads=wkeys("win"))
            rms_feat(0, C_KVG, ("w", sKV), wKV[:].rearrange("p (k f) -> p k f", f=512), 12)
            sW, wW = ring_load([(lambda sl: sl[:].rearrange("p (k f) -> p k f", f=2048),
                                 wb["wkv"].rearrange("(k p) f -> p k f", p=128))], reads=wkeys("wkv"))
            wv = wW[:].rearrange("p (k f) -> p k f", f=2048)
            nkeys = [kL(12), kL(13)]
            if stop and stop.startswith("kv3a"):
                return
            for h in range(8):
                b = bank("u")
                mmg(ps[b][:], [(wv[:, kc, h * 128:(h + 1) * 128], nrm(12, kc)) for kc in range(4)],
                    [("w", sW)] + nkeys, [kps(b)])
                evac_copy(r2[:, h, :], ps[b][:], [kps(b)], [kr2(h)])

            def fnk(e):
                return [e.dma_start(out=KTs[:, :, ti * T:(ti + 1) * T].rearrange("h d t -> d h t"), in_=r2[:, 0:8, :])]
            sc.add("act", fnk, reads=[kr2(h) for h in range(8)], writes=[("KT", ti)], dma="stK", ninc=1)
            if stop == "kv3b":
                return
            vst = r2[:, 8:16, :].rearrange("p (s a) t -> p s (a t)", s=4)
            for s in range(4):
                for hf in range(2):
                    b = bank("d")
                    mmg(ps[b][:], [(nrm(12, kc)[:, s * 128:(s + 1) * 128], wv[:, kc, 1024 + hf * 512:1024 + (hf + 1) * 512])
                                   for kc in range(4)], [("w", sW)] + nkeys, [kps(b)])
                    evac_copy(vst[:, s, hf * 512:(hf + 1) * 512], ps[b][:], [kps(b)], [kr2(8 + 2 * s + hf)])

            def fnv(e):
                return [e.dma_start(out=Vs[:, :, ti * 4 + s, :].rearrange("h p d -> p h d"),
                                    in_=vst[:, s, :].rearrange("p (h d) -> p h d", d=128)) for s in range(4)]
            sc.add("act", fnv, reads=[kr2(g) for g in range(8, 16)], writes=[("V", ti)], dma="stV", ninc=4)

        def kpe_path(ti):
            sP, wP = ring_load([(lambda sl: sl[:, 0:2048].rearrange("p (k f) -> p k f", f=128),
                                 wb["win"][:, 3072:3200].rearrange("(k p) f -> p k f", p=128))], reads=wkeys("win"))
            wpv = wP[:, 0:2048].rearrange("p (k f) -> p k f", f=128)
            pa, pb = bank("g"), bank("u")
            mmg(ps[pa][0:64, :], [(wpv[:, kc, 0:64], xT[:, kc, :]) for kc in range(KC)], [("w", sP)] + xT_keys, [kps(pa)])
            mmg(ps[pb][0:64, :], [(wpv[:, kc, 64:128], xT[:, kc, :]) for kc in range(KC)], [("w", sP)] + xT_keys, [kps(pb)])
            rope_apply(pa, pb, L(11, T, BF16, 64), 1.0, [], [kL(11)])

            def fnp(e):
                return [e.dma_start(out=KPEs[:, ti * T:(ti + 1) * T], in_=L(11, T, BF16, 64))]
            sc.add("act", fnp, reads=[kL(11)], writes=[("KPE", ti)], dma="stP", ninc=1)

        def q_path():
            sQ, wQ = ring_load([(lambda sl: sl[:].rearrange("p (k f) -> p k f", f=512),
                                 wb["win"][:, 2048:2560].rearrange("(k p) f -> p k f", p=128))], reads=wkeys("win"))
            rms_feat(0, C_QG, ("w", sQ), wQ[:].rearrange("p (k f) -> p k f", f=512), 12)
            sW, wW = ring_load([(lambda sl: sl[:].rearrange("p (k f) -> p k f", f=2048),
                                 wb["wq"].rearrange("(k p) f -> p k f", p=128))], reads=wkeys("wq"))
            wv = wW[:].rearrange("p (k f) -> p k f", f=2048)
            nkeys = [kL(12), kL(13)]
            for h in range(8):
                b = bank("d")
                mmg(ps[b][:], [(wv[:, kc, h * 256:h * 256 + 128], nrm(12, kc)) for kc in range(4)],
                    [("w", sW)] + nkeys, [kps(b)])
                act(r2[:, 16 + h, :], ps[b][:], AF.Copy, [kps(b)], [kr2(16 + h)], scale=QSCALE)
                pa, pb = bank("g"), bank("u")
                mmg(ps[pa][0:64, :], [(wv[:, kc, h * 256 + 128:h * 256 + 192], nrm(12, kc)) for kc in range(4)],
                    [("w", sW)] + nkeys, [kps(pa)])
                mmg(ps[pb][0:64, :], [(wv[:, kc, h * 256 + 192:h * 256 + 256], nrm(12, kc)) for kc in range(4)],
                    [("w", sW)] + nkeys, [kps(pb)])
                vop("dve", "tensor_tensor", [kps(pa), kL(0)], [kL(8)], L(8, T, F32, 64), ps[pa][0:64, :], L(0, T, F32, 64), ALU.mult)
                vop("dve", "tensor_tensor", [kps(pb), kL(1)], [kL(9)], L(9, T, F32, 64), ps[pb][0:64, :], L(1, T, F32, 64), ALU.mult)
                vop("dve", "tensor_tensor", [kL(8), kL(9)], [kL(8)], L(8, T, F32, 64), L(8, T, F32, 64), L(9, T, F32, 64), ALU.add)
                vop("dve", "tensor_scalar", [kL(8)], [kr2(24 + h)], r2[0:64, 24 + h, :], L(8, T, F32, 64), QSCALE, None, ALU.mult)

        def lru(ti, own):
            first_own = (ti == nth)
            slots = {}
            for half in range(2):
                sA, wA = ring_load([(lambda sl: sl[:].rearrange("p (k f) -> p k f", f=512),
                                     wb["win"][:, half * 512:(half + 1) * 512].rearrange("(k p) f -> p k f", p=128))],
                                   reads=wkeys("win"))
                slots[("a", half)] = (sA, wA[:].rearrange("p (k f) -> p k f", f=512))
                if own:
                    sG, wG = ring_load([(lambda sl: sl[:].rearrange("p (k f) -> p k f", f=512),
                                         wb["win"][:, 1024 + half * 512:1024 + (half + 1) * 512].rearrange("(k p) f -> p k f", p=128))],
                                       reads=wkeys("win"))
                    slots[("g", half)] = (sG, wG[:].rearrange("p (k f) -> p k f", f=512))
                for cc in range(4):
                    c = half * 4 + cc
                    sA, wAv = slots[("a", half)]
                    b = bank("g")
                    mmg(ps[b][:], [(wAv[:, kc, cc * 128:(cc + 1) * 128], xT[:, kc, :]) for kc in range(KC)],
                        [("w", sA)] + xT_keys, [kps(b)])
                    lin = Lb[:, 2, 0:515]
                    if first_own:
                        vop("dve", "tensor_scalar", ["halo", "cst"], [kL(2)], lin[:, 0:3], halo[:, c, 0:3], flag_ap, None, ALU.mult)
                        vop("dve", "tensor_scalar", ["carry", "cst"], ["carry"], carry[:, c:c + 1], carry[:, c:c + 1], flag_ap, None, ALU.mult)
                    else:
                        vop("dve", "tensor_copy", ["halo"], [kL(2)], lin[:, 0:3], halo[:, c, 0:3])
                    act(lin[:, 3:515], ps[b][:], AF.Copy, [kps(b)], [kL(2)])
                    vop("dve", "tensor_copy", [kL(2)], ["halo"], halo[:, c, 0:3], lin[:, 512:515])
                    act(L(3), lin[:, 3:515], AF.Identity, [kL(2), "cst"], [kL(3)], bias=cv[:, c, 4:5], scale=cv[:, c, 3:4])
                    for k in range(3):
                        vop("dve", "scalar_tensor_tensor", [kL(2), kL(3), "cst"], [kL(3)], L(3), lin[:, k:k + 512],
                            cv[:, c, k:k + 1], L(3), ALU.mult, ALU.add)
                    act(L(4, T, BF16), L(3), AF.Copy, [kL(3)], [kL(4)])
                    ba, bx = bank("u"), bank("d")
                    mmg(ps[ba][:], [(wabd[:, c, :], L(4, T, BF16))], ["wabd", kL(4)], [kps(ba)])
                    mmg(ps[bx][:], [(wxbd[:, c, :], L(4, T, BF16))], ["wxbd", kL(4)], [kps(bx)])
                    act(L(5), ps[ba][:], AF.Sigmoid, [kps(ba), "cst"], [kL(5)], bias=cv[:, c, 5:6])
                    act(L(6), ps[bx][:], AF.Sigmoid, [kps(bx), "cst"], [kL(6)], bias=cv[:, c, 6:7])
                    act(L(7), L(5), AF.Exp, [kL(5), "cl"], [kL(7)], scale=clt[:, 1, c:c + 1])
                    act(L(5), L(5), AF.Exp, [kL(5), "cl"], [kL(5)], scale=clt[:, 2, c:c + 1])
                    act(L(5), L(5), AF.Sqrt, [kL(5)], [kL(5)], bias=1.0, scale=-1.0)
                    vop("dve", "tensor_tensor", [kL(6), kL(3)], [kL(6)], L(6), L(6), L(3), ALU.mult)
                    vop("dve", "tensor_tensor", [kL(6), kL(5)], [kL(6)], L(6), L(6), L(5), ALU.mult)
                    vop("dve", "tensor_tensor_scan", [kL(7), kL(6), "carry"], [kL(8)], L(8), L(7), L(6),
                        carry[:, c:c + 1], ALU.mult, ALU.add)
                    vop("dve", "tensor_copy", [kL(8)], ["carry"], carry[:, c:c + 1], L(8)[:, T - 1:T])
                    if own:
                        sG, wGv = slots[("g", half)]
                        bq = bank("m")
                        mmg(ps[bq][:], [(wGv[:, kc, cc * 128:(cc + 1) * 128], xT[:, kc, :]) for kc in range(KC)],
                            [("w", sG)] + xT_keys, [kps(bq)])
                        act(L(9), ps[bq][:], AF.Copy, [kps(bq)], [kL(9)])
                        vop("dve", "tensor_tensor", [kL(9)], [kL(10)], L(10), L(9), L(9), ALU.mult)
                        vop("dve", "tensor_scalar", [kL(10)], [kL(10)], L(10), L(10), 0.044715, 1.0, ALU.mult, ALU.add)
                        vop("dve", "tensor_tensor", [kL(10), kL(9)], [kL(10)], L(10), L(10), L(9), ALU.mult)
                        act(L(10), L(10), AF.Sigmoid, [kL(10)], [kL(10)], scale=1.5957691216057308)
                        vop("dve", "tensor_tensor", [kL(10), kL(9)], [kL(10)], L(10), L(10), L(9), ALU.mult)
                        vop("dve", "tensor_tensor", [kL(10), kL(8)], [kr2(c)], r2[:, c, :], L(10), L(8), ALU.mult)

        def out_proj():
            for dc in range(4):
                sW, wW = ring_load([(lambda sl: sl[:].rearrange("p (k f) -> p k f", f=512),
                                     wb["wout"][:, dc * 512:(dc + 1) * 512].rearrange("(k p) f -> p k f", p=128))],
                                   reads=wkeys("wout"))
                wv = wW[:].rearrange("p (k f) -> p k f", f=512)
                for s in range(4):
                    b = bank("d")
                    mmg(ps[b][:], [(r2[:, kc, s * 128:(s + 1) * 128], wv[:, kc, :]) for kc in range(KC)],
                        [("w", sW)] + [kr2(g) for g in range(16)], [kps(b)])
                    vop("dve", "scalar_tensor_tensor", [kps(b), ("xtm", s, dc)], [("xtm", s, dc)],
                        xtm[:, s, dc * 512:(dc + 1) * 512], xtm[:, s, dc * 512:(dc + 1) * 512], ALPHA, ps[b][:],
                        ALU.mult, ALU.add)

        def ple(to):
            def fnp(e):
                return [e.dma_start(out=Lb[:, a, 0:512].rearrange("p (s c) -> p s c", c=256),
                                    in_=ps_in[to * T + a * 256:to * T + (a + 1) * 256, :].rearrange("(s p) c -> p s c", p=128))
                        for a in range(2)]
            sc.add("sp", fnp, reads=[], writes=[kL(0), kL(1)], dma="p", ninc=2)

            class _PV:
                def __getitem__(self, idx):
                    _, s, cs = idx
                    return Lb[:, s // 2, (s % 2) * 256 + cs.start:(s % 2) * 256 + cs.stop]
            ptv = _PV()
            pT = Lb[:, 2, :].bitcast(BF16)[:, 0:1024].rearrange("p (k t) -> p k t", t=512)
            for k2 in range(2):
                b = bank("m")

                def fn(e, k2=k2, b=b):
                    inst = None
                    for s in range(4):
                        inst = e.transpose(ps[b][:, s * 128:(s + 1) * 128], ptv[:, s, k2 * 128:(k2 + 1) * 128], ident32)
                    return inst
                sc.add("pe", fn, reads=[kL(0), kL(1), "cst"], writes=[kps(b)])
                vop("dve", "tensor_copy", [kps(b)], [kL(2)], pT[:, k2, :], ps[b][:])
            if stop == "ple1":
                return
            sB, wB_ = ring_load([(lambda sl: sl[:, 0:4096].rearrange("p (k f) -> p k f", f=2048),
                                  wb["wpp"].rearrange("(k p) f -> p k f", p=128)),
                                 (lambda sl: sl[:].bitcast(F32)[:, 2048:4096].rearrange("p (o n) -> p o n", o=1),
                                  bg_in[0:1, :].partition_broadcast(128))], reads=wkeys("wpp"))
            wpp = wB_[:, 0:4096].rearrange("p (k f) -> p k f", f=2048)
            bgB = wB_[:].bitcast(F32)[:, 2048:4096]
            if stop == "ple2":
                vop("dve", "tensor_copy", [("w", sB)], [kL(3)], L(3), bgB[:, 0:512])
                return
            for dc in range(1 if stop == "ple3" else 4):
                sW, wW = ring_load([(lambda sl: sl[:].rearrange("p (k f) -> p k f", f=512),
                                     wb["wpg"][:, dc * 512:(dc + 1) * 512].rearrange("(k p) f -> p k f", p=128))],
                                   reads=wkeys("wpg"))
                wv = wW[:].rearrange("p (k f) -> p k f", f=512)
                for s in range(4):
                    bgk, bpk = bank("g"), bank("u")
                    mmg(ps[bgk][:], [(xT[:, kc, s * 128:(s + 1) * 128], wv[:, kc, :]) for kc in range(KC)],
                        [("w", sW)] + xT_keys, [kps(bgk)])
                    mmg(ps[bpk][:], [(pT[:, k2, s * 128:(s + 1) * 128], wpp[:, k2, dc * 512:(dc + 1) * 512]) for k2 in range(2)],
                        [("w", sB), kL(2)], [kps(bpk)])
                    lj = 3 + (s % 2)
                    vop("dve", "tensor_tensor", [kps(bgk), ("w", sB)], [kL(lj)], L(lj), ps[bgk][:], bgB[:, dc * 512:(dc + 1) * 512], ALU.add)
                    act(L(lj), L(lj), AF.Sigmoid, [kL(lj)], [kL(lj)])
                    vop("dve", "tensor_tensor", [kL(lj), kps(bpk)], [kL(lj)], L(lj), L(lj), ps[bpk][:], ALU.mult)
                    vop("dve", "scalar_tensor_tensor", [kL(lj), ("xtm", s, dc)], [("xtm", s, dc)],
                        xtm[:, s, dc * 512:(dc + 1) * 512], xtm[:, s, dc * 512:(dc + 1) * 512], ALPHA, L(lj),
                        ALU.mult, ALU.add)

        def store_out(to):
            def fn(e):
                return [e.dma_start(out=yout[to * T:(to + 1) * T, :].rearrange("(s p) d -> p s d", p=128), in_=xtm[:])]
            sc.add("act", fn, reads=xtm_keys, writes=[("y", to)], dma="sty", ninc=1)

        def attention_full(ti):
            nkc = ti + 1
            ystage = [Lb[:, 4, :].bitcast(BF16)[:, 0:1024], Lb[:, 5, :].bitcast(BF16)[:, 0:1024]]
            for h in range(8):
                nk = nkc * T
                sK, wKt = ring_load([(lambda sl, nk=nk: sl[:, 0:nk], KTs[h, :, 0:nk])], reads=[("KT", j) for j in range(nkc)])
                sP, wPt = ring_load([(lambda sl, nk=nk: sl[0:64, 0:nk], KPEs[:, 0:nk])], reads=[("KPE", j) for j in range(nkc)])
                sV, wVt = ring_load([(lambda sl, nkc=nkc: sl[:, 0:nkc * 512].rearrange("p (b d) -> p b d", d=128),
                                      Vs[h, :, 0:nkc * 4, :])], reads=[("V", j) for j in range(nkc)])
                wK, wP = wKt, wPt
                wVv = wVt[:].rearrange("p (b d) -> p b d", d=128)
                for qb in range(4):
                    qs = slice(qb * 128, (qb + 1) * 128)
                    qn = r2[:, 16 + h, qs]
                    qp = r2[0:64, 24 + h, qs]
                    ndiag = (qb + 1) * 128

                    def scores(b, kcx, n, qn=qn, qp=qp, wK=wK, wP=wP, sK=sK, sP=sP, h=h):
                        mmg(ps[b][:, 0:n], [(qn, wK[:, kcx * T:kcx * T + n]), (qp, wP[0:64, kcx * T:kcx * T + n])],
                            [("w", sK), ("w", sP), kr2(16 + h), kr2(24 + h)], [kps(b)])
                    for kcx in range(nkc):
                        diag = (kcx == ti)
                        n = ndiag if diag else T
                        b = bank("g")
                        scores(b, kcx, n)
                        if diag:
                            vop("dve", "tensor_tensor", [kps(b), "cst"], [kL(6)], L(6)[:, 0:n], ps[b][:, 0:n],
                                cm3[:, (3 - qb) * 128:(3 - qb) * 128 + n], ALU.add)
                            vop("dve", "reduce_max", [kL(6)], ["mxt"], mxt[:, kcx:kcx + 1], L(6)[:, 0:n], AX.X)
                        else:
                            vop("dve", "reduce_max", [kps(b)], ["mxt"], mxt[:, kcx:kcx + 1], ps[b][:, 0:n], AX.X)
                    vop("dve", "tensor_scalar", ["mxt", "cst"], ["mxt"], mxt[:, 0:nth], mxt[:, 0:nth], mbias_ap, None, ALU.add)
                    vop("dve", "reduce_max", ["mxt"], ["mxm"], sm[:, 8:9], mxt[:, 0:nkc], AX.X)
                    vop("dve", "tensor_scalar", ["mxm"], ["negm"], sm[:, 9:10], sm[:, 8:9], -1.0, None, ALU.mult)
                    vop("dve", "tensor_scalar", ["negm", "cst"], ["negmo"], sm[:, 10:11], sm[:, 9:10], mbias_ap, None, ALU.add)
                    bo = bank("d")
                    nblk_tot = (nkc - 1) * 4 + (qb + 1)
                    blk_i = 0
                    for kcx in range(nkc):
                        diag = (kcx == ti)
                        n = ndiag if diag else T
                        pj = 7 + (kcx % 2)
                        Pt = L(pj, T, BF16)
                        rs = mxt[:, 20 + (kcx % 20):21 + (kcx % 20)]
                        if diag:
                            act(Pt[:, 0:n], L(6)[:, 0:n], AF.Exp, [kL(6), "negm"], [kL(pj), ("rs", kcx)], bias=sm[:, 9:10], accum_out=rs)
                        else:
                            b = bank("g")
                            scores(b, kcx, n)
                            act(Pt[:, 0:n], ps[b][:, 0:n], AF.Exp, [kps(b), "negm", "negmo"], [kL(pj), ("rs", kcx)],
                                bias=(sm[:, 10:11] if kcx < nth else sm[:, 9:10]), accum_out=rs)
                        nb = n // 128
                        bt = bank("u")
                        ptp = ps[bt][:].bitcast(BF16)

                        def fnt(e, Pt=Pt, nb=nb, ptp=ptp):
                            inst = None
                            for jb in range(nb):
                                inst = e.transpose(ptp[:, jb * 128:(jb + 1) * 128], Pt[:, jb * 128:(jb + 1) * 128], identb[:])
                            return inst
                        sc.add("pe", fnt, reads=[kL(pj), "identb"], writes=[kps(bt)])
                        tj = 9 + (kcx % 2)
                        PT = L(tj, T, BF16)
                        evac_copy(PT[:, 0:n], ptp[:, 0:n], [kps(bt)], [kL(tj)])

                        def fnpv(e, PT=PT, nb=nb, kcx=kcx, blk_i=blk_i, bo=bo, nblk_tot=nblk_tot, wVv=wVv):
                            inst = None
                            for jb in range(nb):
                                gi = blk_i + jb
                                inst = e.matmul(ps[bo][:, 0:128], PT[:, jb * 128:(jb + 1) * 128], wVv[:, kcx * 4 + jb, :],
                                                start=(gi == 0), stop=(gi == nblk_tot - 1))
                            return inst
                        sc.add("pe", fnpv, reads=[kL(tj), ("w", sV)], writes=[kps(bo)])
                        blk_i += nb
                    vop("dve", "reduce_sum", [("rs", k) for k in range(nkc)], ["lsum"], sm[:, 11:12], mxt[:, 20:20 + nkc], AX.X)
                    vop("dve", "reciprocal", ["lsum"], ["linv"], sm[:, 12:13], sm[:, 11:12])
                    yj = 11 + (qb % 2)
                    vop("dve", "tensor_scalar", [kps(bo), "linv"], [kL(yj)], L(yj, 128, BF16), ps[bo][:, 0:128], sm[:, 12:13], None, ALU.mult)
                    bt = bank("u")
                    ptp = ps[bt][:].bitcast(BF16)

                    def fny(e, yj=yj, ptp=ptp):
                        return e.transpose(ptp[:, 0:128], L(yj, 128, BF16), identb[:])
                    sc.add("pe", fny, reads=[kL(yj), "identb"], writes=[kps(bt)])
                    evac_copy(r2[:, 8 + h, qs], ptp[:, 0:128], [kps(bt)], [kr2(8 + h)])

        order = ["x", "tr", "ffn", "ln1", "kv", "lru", "attn", "ln2", "ffn2", "full"]
        lvl = (4 if stop.startswith("kv") else 9 if stop.startswith("ple") else order.index(stop)) if stop else len(order) - 1

        def tile_prog(ti):
            own = ti >= nth
            to = ti - nth
            load_x(ti)
            if lvl < 1:
                return
            transposes()
            prescale()
            if lvl < 2:
                return
            ffn("w1g", "w1u", "w1d")
            if lvl < 3:
                return
            layernorm(0)
            if lvl < 4:
                return
            transposes()
            if stop == "kv0":
                return
            rope_tables(ti)
            if stop == "kv1":
                return
            kpe_path(ti)
            if stop == "kv2":
                return
            kv_path(ti)
            if stop and stop.startswith("kv3"):
                return
            if own:
                q_path()
            if lvl < 5:
                return
            lru(ti, own)
            if not own or lvl < 6:
                return
            attention_full(ti)
            if lvl < 7:
                return
            out_proj()
            layernorm(1)
            if lvl < 8:
                return
            transposes()
            prescale()
            ffn("w2g", "w2u", "w2d")
            layernorm(2)
            if lvl < 9:
                return
            transposes()
            ple(to)
            if stop and stop.startswith("ple"):
                return
            layernorm(3)

        for ti in range(NTILE):
            tile_prog(ti)
            if ti >= nth:
                store_out(ti - nth)
        sc.add("sp", None, reads=[("y", t_) for t_ in range(nth)], writes=[])
        sc.emit(nc, stack)
    return nc


def _host_layouts(inp, S):
    f = lambda a: np.ascontiguousarray(np.asarray(a, dtype=np.float32))
    w = {}
    w["w1g"], w["w1u"], w["w1d"] = f(inp["ffn1_w_gate"][0]), f(inp["ffn1_w_up"][0]), f(inp["ffn1_w_down"][0])
    w["w2g"], w["w2u"], w["w2d"] = f(inp["ffn2_w_gate"][0]), f(inp["ffn2_w_up"][0]), f(inp["ffn2_w_down"][0])
    win = f(inp["w_in"][0])
    kpe = win[:, 3072:3136]
    w["win"] = np.ascontiguousarray(np.concatenate([win, kpe[:, 32:64], kpe[:, 0:32]], axis=1))
    wq = f(inp["w_q_up"][0]).reshape(512, 8, 192)
    w["wq"] = np.ascontiguousarray(np.concatenate([wq, wq[:, :, 160:192], wq[:, :, 128:160]], axis=2).reshape(512, 2048))
    wkv = f(inp["w_kv_up"][0]).reshape(512, 8, 256)
    w["wkv"] = np.ascontiguousarray(np.concatenate([wkv[:, :, 0:128].reshape(512, 1024), wkv[:, :, 128:256].reshape(512, 1024)], axis=1))
    w["wout"], w["wpg"], w["wpp"] = f(inp["w_out"][0]), f(inp["ple_w_gate"][0]), f(inp["ple_w_proj"][0])
    lnp = np.stack([np.concatenate([f(inp[f"ln{i}_g"][0]), f(inp[f"ln{i}_b"][0])]) for i in (1, 2, 3, 4)])
    bg = f(inp["ple_b_gate"][0]).reshape(1, 2048)
    cst = np.zeros((128, NCST), np.float32)
    cvv = np.zeros((128, 8, 8), np.float32)
    chan = lambda v: f(v).reshape(8, 128).T
    cw = f(inp["conv_w"][0])
    for k in range(4):
        cvv[:, :, k] = chan(cw[k])
    cvv[:, :, 4] = chan(inp["conv_b"][0])
    cvv[:, :, 5] = chan(f(inp["lru_b_a"][0]).reshape(-1))
    cvv[:, :, 6] = chan(f(inp["lru_b_x"][0]).reshape(-1))
    cvv[:, :, 7] = chan(inp["lru_lambda"][0])
    cst[:, C_CV:C_CV + 64] = cvv.reshape(128, 64)
    cst[:, C_QG:C_QG + 4] = f(inp["q_norm_g"][0]).reshape(4, 128).T
    cst[:, C_KVG:C_KVG + 4] = f(inp["kv_norm_g"][0]).reshape(4, 128).T
    invf = (10000.0 ** (-np.arange(0, 64, 2, dtype=np.float32) / np.float32(64))).astype(np.float32)
    cst[0:64, C_ROPE] = np.tile(invf, 2)
    cst[0:64, C_ROPE + 1] = np.concatenate([-np.ones(32, np.float32), np.ones(32, np.float32)])
    cst[:, C_ID:C_ID + 128] = np.eye(128, dtype=np.float32)
    qi = np.arange(128)[:, None]
    kj = np.arange(512)[None, :]
    cst[:, C_CM:C_CM + 512] = np.where(kj <= qi + 384, 0.0, NEG).astype(np.float32)
    wbd = np.zeros((128, 2, 8, 128), np.float32)
    wa, wx = f(inp["lru_w_a"][0]), f(inp["lru_w_x"][0])
    for c in range(8):
        for hb in range(2):
            wbd[hb * 64:(hb + 1) * 64, 0, c, hb * 64:(hb + 1) * 64] = wa[2 * c + hb]
            wbd[hb * 64:(hb + 1) * 64, 1, c, hb * 64:(hb + 1) * 64] = wx[2 * c + hb]
    shared = dict(w, lnp=lnp, bg=bg, wbd=wbd.reshape(128, 2048))
    x = f(inp["x"])
    p = f(inp["p"][0])
    posn = np.asarray(inp["positions"]).astype(np.int32)
    S2 = S // 2
    maps = []
    for c in range(8):
        b, h = c // 2, c % 2
        own = slice(h * S2, (h + 1) * S2)
        oth = slice((1 - h) * S2, (2 - h) * S2)
        cc = cst.copy()
        cc[:, C_FLAG] = 1.0 if h == 1 else 0.0
        cc[:, C_FLAG + 1] = 0.0 if h == 1 else NEG
        m = dict(shared)
        m["xs"] = np.ascontiguousarray(np.concatenate([x[b, oth], x[b, own]], axis=0))
        m["ps"] = np.ascontiguousarray(p[b, own])
        m["pos"] = np.ascontiguousarray(np.concatenate([posn[b, oth], posn[b, own]])[None, :])
        m["cst"] = cc
        maps.append(m)
    return maps


_NC_CACHE = {}


def kernel(**inputs):
    x = np.asarray(inputs["x"])
    B, S, _ = x.shape
    nth = S // 2 // T
    inputs = dict(inputs)
    stop = inputs.pop("_stop", None)
    key = (nth, stop)
    if key not in _NC_CACHE:
        _NC_CACHE[key] = build(nth, stop)
    nc = _NC_CACHE[key]
    ncores = inputs.pop("_ncores", 8)
    maps = _host_layouts(inputs, S)[:ncores]
    res = run_bass_kernel_spmd(nc, maps, core_ids=list(range(ncores)))
    out = np.zeros((B, S, D), np.float32)
    S2 = S // 2
    for c in range(ncores):
        b, h = c // 2, c % 2
        out[b, h * S2:(h + 1) * S2] = np.asarray(res.results[c]["y"], dtype=np.float32).reshape(S2, D)
    return out
```

```python
import numpy as np
import concourse.bass as bass
import concourse.mybir as mybir
from concourse.bass_utils import run_bass_kernel_spmd

F32 = mybir.dt.float32
BF16 = mybir.dt.bfloat16
I32 = mybir.dt.int32
AF = mybir.ActivationFunctionType
ALU = mybir.AluOpType
AX = mybir.AxisListType

T = 512
D = 2048
KC = 16
DFF = 5632
NFG = 11
ALPHA = 2.0 ** 0.25
LN_EPS = 1e-5
RMS_EPS = 1e-6
QSCALE = 192.0 ** -0.5
NSLOT = 5
LSZ = 528
NL = 14
NEG = -30000.0
PI = float(np.pi)

C_CV = 0
C_QG = 64
C_KVG = 68
C_FLAG = 72
C_ROPE = 74
C_ID = 76
C_CM = 204
NCST = 716


class _Op:
    __slots__ = ("eng", "fn", "deps", "signal", "sigval", "dma", "ninc")


class Sched:
    def __init__(self):
        self.ops = []
        self.lastw = {}
        self.readers = {}

    def add(self, eng, fn, reads=(), writes=(), dma=None, ninc=1):
        idx = len(self.ops)
        deps = {}

        def dep(j):
            o = self.ops[j]
            if o.dma is None and o.eng == "pe" and eng == "pe" and dma is None:
                return
            k = ("d", o.dma) if o.dma is not None else ("e", o.eng)
            if k not in deps or deps[k] < j:
                deps[k] = j

        for k in reads:
            w = self.lastw.get(k)
            if w is not None:
                dep(w)
        for k in writes:
            w = self.lastw.get(k)
            if w is not None:
                dep(w)
            for r in self.readers.get(k, ()):
                dep(r)
        for k in writes:
            self.lastw[k] = idx
            self.readers[k] = []
        for k in reads:
            if k not in writes:
                self.readers.setdefault(k, []).append(idx)
        op = _Op()
        op.eng, op.fn, op.dma, op.ninc = eng, fn, dma, ninc
        op.deps = list(deps.values())
        op.signal = False
        op.sigval = 0
        for j in op.deps:
            self.ops[j].signal = True
        self.ops.append(op)
        return idx

    def emit(self, nc, stack):
        engs = ("pe", "act", "dve", "pool", "sp")
        esem = {e: stack.enter_context(nc.semaphore("s_" + e)) for e in engs}
        dsem = {}
        ecnt = {e: 0 for e in engs}
        dcnt = {}
        for op in self.ops:
            if op.dma is not None:
                if op.dma not in dsem:
                    dsem[op.dma] = stack.enter_context(nc.semaphore("d_" + op.dma))
                    dcnt[op.dma] = 0
                dcnt[op.dma] += 16 * op.ninc
                op.sigval = dcnt[op.dma]
            elif op.signal:
                ecnt[op.eng] += 1
                op.sigval = ecnt[op.eng]
        ops = self.ops
        block = stack.enter_context(nc.Block())

        def run(e, name):
            waited = {}
            for op in ops:
                if op.eng != name:
                    continue
                for j in op.deps:
                    p = ops[j]
                    sem = dsem[p.dma] if p.dma is not None else esem[p.eng]
                    key = id(sem)
                    if waited.get(key, 0) < p.sigval:
                        e.wait_ge(sem, p.sigval)
                        waited[key] = p.sigval
                if op.fn is None:
                    continue
                r = op.fn(e)
                if op.dma is not None:
                    for inst in r:
                        inst.then_inc(dsem[op.dma], 16)
                elif op.signal:
                    inst = r[-1] if isinstance(r, (list, tuple)) else r
                    inst.then_inc(esem[name], 1)

        @block.tensor
        def _(e):
            run(e, "pe")

        @block.scalar
        def _(e):
            run(e, "act")

        @block.vector
        def _(e):
            run(e, "dve")

        @block.gpsimd
        def _(e):
            run(e, "pool")

        @block.sync
        def _(e):
            run(e, "sp")


def build(nth, stop=None):
    import contextlib

    nc = bass.Bass("TRN2", target_bir_lowering=False)
    S2 = nth * T
    NTOK = 2 * S2
    NTILE = 2 * nth
    NBLK = NTOK // 128

    def din(name, shape, dt=F32):
        return nc.dram_tensor(name, shape, dt, kind="ExternalInput").ap()

    def dint(name, shape, dt=BF16):
        return nc.dram_tensor(name, shape, dt, kind="Internal").ap()

    xs = din("xs", [NTOK, D])
    ps_in = din("ps", [S2, 256])
    pos = din("pos", [1, NTOK], I32)
    cst_in = din("cst", [128, NCST])
    wbd_in = din("wbd", [128, 2048])
    lnp = din("lnp", [4, 4096])
    bg_in = din("bg", [1, 2048])
    wshapes = dict(w1g=[D, DFF], w1u=[D, DFF], w1d=[DFF, D], win=[D, 3200], wkv=[512, 2048],
                   wq=[512, 2048], wout=[D, D], w2g=[D, DFF], w2u=[D, DFF], w2d=[DFF, D],
                   wpg=[D, D], wpp=[256, D])
    w32 = {k: din(k, v) for k, v in wshapes.items()}
    wb = {k: dint(k + "_b", v) for k, v in wshapes.items()}
    KTs = dint("KTs", [8, 128, NTOK])
    KPEs = dint("KPEs", [64, NTOK])
    Vs = dint("Vs", [8, 128, NBLK, 128])
    yout = nc.dram_tensor("y", [S2, D], F32, kind="ExternalOutput").ap()

    sc = Sched()
    stack = contextlib.ExitStack()
    with stack:
        def sb(name, shape, dt):
            return stack.enter_context(nc.sbuf_tensor(name, shape, dt))

        xtm = sb("xtm", [128, 4, D], F32)
        xT = sb("xT", [128, KC, T], BF16)
        r2 = sb("r2", [128, 32, T], BF16)
        wr = [sb(f"wr{i}", [128, 8192], BF16) for i in range(NSLOT)]
        Lb = sb("Lb", [128, NL, LSZ], F32)
        cst = sb("cst_sb", [128, NCST], F32)
        wabd = sb("wabd", [128, 8, 128], BF16)
        wxbd = sb("wxbd", [128, 8, 128], BF16)
        identb = sb("identb", [128, 128], BF16)
        onesb = sb("onesb", [128, 128], BF16)
        clt = sb("clt", [128, 3, 8], F32)
        halo = sb("halo", [128, 8, 4], F32)
        carry = sb("carry", [128, 8], F32)
        sm = sb("sm", [128, 64], F32)
        bnst4 = sb("bnst4", [128, 4, 24], F32)
        mxt = sb("mxt", [128, 40], F32)
        ps = [stack.enter_context(nc.psum_tensor(f"ps{i}", [128, 512], F32)) for i in range(8)]

        ident32 = cst[:, C_ID:C_ID + 128]
        cm3 = cst[:, C_CM:C_CM + 512]
        cv = cst[:, C_CV:C_CV + 64].rearrange("p (c j) -> p c j", j=8)
        flag_ap = cst[:, C_FLAG:C_FLAG + 1]
        mbias_ap = cst[:, C_FLAG + 1:C_FLAG + 2]
        invf_ap = cst[0:64, C_ROPE:C_ROPE + 1]
        sgn_ap = cst[0:64, C_ROPE + 1:C_ROPE + 2]

        def L(j, n=T, dt=F32, p=128):
            a = Lb[0:p, j, :]
            if dt == BF16:
                return a.bitcast(BF16)[:, 0:n]
            if dt == I32:
                return a.bitcast(I32)[:, 0:n]
            return a[:, 0:n]

        def kL(j):
            return ("L", j)

        bank_ctr = {}

        def bank(group):
            base = dict(g=0, u=2, d=4, m=6, dd=4)[group]
            i = bank_ctr.get(group, 0)
            bank_ctr[group] = i + 1
            return base + (i % (4 if group == "dd" else 2))

        def kps(b):
            return ("ps", b)

        xtm_keys = [("xtm", s, dc) for s in range(4) for dc in range(4)]
        xT_keys = [("xT", kc) for kc in range(KC)]

        def kr2(g):
            return ("r2", g)

        def mmg(out, pairs, reads, writes):
            def fn(e, out=out, pairs=pairs):
                n = len(pairs)
                inst = None
                for i, (l, r) in enumerate(pairs):
                    inst = e.matmul(out, l, r, start=(i == 0), stop=(i == n - 1))
                return inst
            sc.add("pe", fn, reads, writes)

        def act(out, in_, func, reads, writes, bias=None, scale=None, accum_out=None):
            def fn(e):
                kw = {}
                if bias is not None:
                    kw["bias"] = bias
                if scale is not None:
                    kw["scale"] = scale
                if accum_out is not None:
                    kw["accum_out"] = accum_out
                return e.activation(out, in_, func, **kw)
            sc.add("act", fn, reads, writes)

        def vop(eng, name, reads, writes, *a, **kw):
            def fn(e):
                return getattr(e, name)(*a, **kw)
            sc.add(eng, fn, reads, writes)

        ring_ctr = [0]

        def ring_load(dmas, reads):
            s = ring_ctr[0] % NSLOT
            ring_ctr[0] += 1
            slot = wr[s]

            def fn(e, dmas=dmas, slot=slot):
                return [e.dma_start(out=mk(slot), in_=src) for mk, src in dmas]
            sc.add("sp", fn, reads=reads, writes=[("w", s)], dma=f"w{s}", ninc=len(dmas))
            return s, slot

        def const_fn(e):
            return [e.dma_start(out=cst[:], in_=cst_in[:, :]),
                    e.dma_start(out=Lb[:, 0:4, 0:512], in_=wbd_in.rearrange("p (a b) -> p a b", b=512))]
        sc.add("sp", const_fn, reads=[], writes=["cst"] + [kL(j) for j in range(4)], dma="cst", ninc=2)

        cv_ctr = [0]

        def wkeys(name):
            return [("wb", name, r0) for r0 in range(0, wshapes[name][0], 256)]

        def conv_weight(name):
            src, dst = w32[name], wb[name]
            rows, cols = wshapes[name]
            step = 256
            for r0 in range(0, rows, step):
                r1 = min(rows, r0 + step)
                ch = cv_ctr[0] % 4
                cv_ctr[0] += 1

                def fn(e, src=src, dst=dst, r0=r0, r1=r1):
                    return [e.dma_start(out=dst[r0:r1, :], in_=src[r0:r1, :], max_dma_last_dim=2048)]
                sc.add("pool", fn, reads=[], writes=[("wb", name, r0), ("cvchain", ch)], dma=f"cv{ch}", ninc=1)

        for name in ("w1g", "w1u", "w1d", "win", "wkv", "wq", "wout", "w2g", "w2u", "w2d", "wpg", "wpp"):
            conv_weight(name)

        vop("dve", "memset", [], ["onesb"], onesb[:], 1.0)
        vop("dve", "tensor_copy", ["cst"], ["identb"], identb[:], ident32)
        vop("dve", "tensor_copy", [kL(0), kL(1)], ["wabd"], wabd[:].rearrange("p (a c) f -> p a (c f)", a=2),
            Lb[:, 0:2, 0:512])
        vop("dve", "tensor_copy", [kL(2), kL(3)], ["wxbd"], wxbd[:].rearrange("p (a c) f -> p a (c f)", a=2),
            Lb[:, 2:4, 0:512])
        vop("dve", "memset", [], ["halo"], halo[:], 0.0)
        vop("dve", "memset", [], ["carry"], carry[:], 0.0)
        act(clt[:, 0, :], cv[:, :, 7], AF.Exp, ["cst"], ["clt0"], scale=-1.0)
        act(clt[:, 0, :], clt[:, 0, :], AF.Ln, ["clt0"], ["clt0"], bias=1.0)
        vop("dve", "tensor_scalar", ["clt0"], ["cl"], clt[:, 1, :], clt[:, 0, :], -8.0, None, ALU.mult)
        vop("dve", "tensor_scalar", ["clt0"], ["cl"], clt[:, 2, :], clt[:, 0, :], -16.0, None, ALU.mult)

        def load_x(ti):
            src = xs[ti * T:(ti + 1) * T, :].rearrange("(s p) d -> p s d", p=128)

            def fn(e, src=src):
                return [e.dma_start(out=xtm[:], in_=src)]
            sc.add("sp", fn, reads=[], writes=xtm_keys, dma="x", ninc=1)

        evac_ctr = [0]

        def evac_copy(out, in_, reads, writes):
            evac_ctr[0] += 1
            if evac_ctr[0] % 2:
                act(out, in_, AF.Copy, reads, writes)
            else:
                vop("dve", "tensor_copy", reads, writes, out, in_)

        def transposes():
            for kc in range(KC):
                b = bank("m")
                pt = ps[b]

                def fn(e, kc=kc, pt=pt):
                    inst = None
                    for s in range(4):
                        inst = e.transpose(pt[:, s * 128:(s + 1) * 128], xtm[:, s, kc * 128:(kc + 1) * 128], ident32)
                    return inst
                sc.add("pe", fn, reads=[("xtm", s, kc // 4) for s in range(4)] + ["cst"], writes=[kps(b)])
                evac_copy(xT[:, kc, :], pt[:], [kps(b)], [("xT", kc)])

        def prescale():
            for s in range(4):
                act(xtm[:, s, :], xtm[:, s, :], AF.Copy, [("xtm", s, dc) for dc in range(4)], [("xtm", s, dc) for dc in range(4)], scale=ALPHA)

        def layernorm(n):
            s_, slot = ring_load([(lambda sl: sl[:].bitcast(F32)[:, 0:4096].rearrange("p (o n) -> p o n", o=1),
                                   lnp[n:n + 1, :].partition_broadcast(128))], reads=[])
            gb = slot[:].bitcast(F32)
            for s in range(4):
                keys = [("xtm", s, dc) for dc in range(4)]
                c0 = 16 + 8 * s
                st = ("lnst", s)
                for dc in range(4):
                    vop("dve", "bn_stats", keys[dc:dc + 1], [("bnst", s, dc)], bnst4[:, s, dc * 6:(dc + 1) * 6], xtm[:, s, dc * 512:(dc + 1) * 512])
                vop("dve", "bn_aggr", [("bnst", s, dc) for dc in range(4)], [st], sm[:, c0:c0 + 2], bnst4[:, s, :])
                vop("dve", "tensor_scalar", [st], [st], sm[:, c0 + 2:c0 + 3], sm[:, c0 + 1:c0 + 2], LN_EPS, None, ALU.add)
                act(sm[:, c0 + 3:c0 + 4], sm[:, c0 + 2:c0 + 3], AF.Sqrt, [st], [st])
                vop("dve", "reciprocal", [st], [st], sm[:, c0 + 4:c0 + 5], sm[:, c0 + 3:c0 + 4])
                vop("dve", "tensor_scalar", [st], [st], sm[:, c0 + 5:c0 + 6], sm[:, c0:c0 + 1], sm[:, c0 + 4:c0 + 5], -1.0, ALU.mult, ALU.mult)
                act(xtm[:, s, :], xtm[:, s, :], AF.Identity, keys + [st], keys, bias=sm[:, c0 + 5:c0 + 6], scale=sm[:, c0 + 4:c0 + 5])
                vop("dve", "tensor_tensor", keys + [("w", s_)], keys, xtm[:, s, :], xtm[:, s, :], gb[:, 0:2048], ALU.mult)
                vop("pool", "tensor_tensor", keys + [("w", s_)], keys, xtm[:, s, :], xtm[:, s, :], gb[:, 2048:4096], ALU.add)

        def ffn(ng, nu, nd):
            wg, wu, wd = wb[ng], wb[nu], wb[nd]
            pend = {}

            def gu(fg):
                sA, wA = ring_load([(lambda sl: sl[:].rearrange("p (k f) -> p k f", f=512),
                                     wg[:, fg * 512:(fg + 1) * 512].rearrange("(k p) f -> p k f", p=128))],
                                   reads=wkeys(ng))
                sB, wB = ring_load([(lambda sl: sl[:].rearrange("p (k f) -> p k f", f=512),
                                     wu[:, fg * 512:(fg + 1) * 512].rearrange("(k p) f -> p k f", p=128))],
                                   reads=wkeys(nu))
                wAv = wA[:].rearrange("p (k f) -> p k f", f=512)
                wBv = wB[:].rearrange("p (k f) -> p k f", f=512)
                for j in range(4):
                    bgk, buk = bank("g"), bank("u")
                    mmg(ps[bgk][:], [(wAv[:, kc, j * 128:(j + 1) * 128], xT[:, kc, :]) for kc in range(KC)],
                        [("w", sA)] + xT_keys, [kps(bgk)])
                    mmg(ps[buk][:], [(wBv[:, kc, j * 128:(j + 1) * 128], xT[:, kc, :]) for kc in range(KC)],
                        [("w", sB)] + xT_keys, [kps(buk)])
                    lj = (fg * 4 + j) % 2
                    act(L(lj), ps[bgk][:], AF.Silu, [kps(bgk)], [kL(lj)])
                    g = (fg % 2) * 4 + j
                    vop("dve", "tensor_tensor", [kL(lj), kps(buk)], [kr2(g)], r2[:, g, :], L(lj), ps[buk][:], ALU.mult)

            def down(fg):
                sD, wD = ring_load([(lambda sl: sl[:].rearrange("p (j d) -> p j d", d=2048),
                                     wd[fg * 512:(fg + 1) * 512, :].rearrange("(j p) d -> p j d", p=128))],
                                   reads=wkeys(nd))
                wDv = wD[:].rearrange("p (j d) -> p j d", d=2048)
                gr = [(fg % 2) * 4 + j for j in range(4)]
                for s in range(4):
                    for dc in range(4):
                        b = bank("dd")
                        mmg(ps[b][:], [(r2[:, gr[j], s * 128:(s + 1) * 128], wDv[:, j, dc * 512:(dc + 1) * 512])
                                       for j in range(4)],
                            [("w", sD)] + [kr2(g) for g in gr], [kps(b)])
                        vop("dve", "scalar_tensor_tensor", [kps(b), ("xtm", s, dc)], [("xtm", s, dc)],
                            xtm[:, s, dc * 512:(dc + 1) * 512], ps[b][:], 0.5,
                            xtm[:, s, dc * 512:(dc + 1) * 512], ALU.mult, ALU.add)

            gu(0)
            for fg in range(NFG):
                if fg + 1 < NFG:
                    gu(fg + 1)
                down(fg)

        def rms_feat(c0, gcol, src_slot_key, wslot_view, outL):
            bss = bank("m")
            for c in range(4):
                b = bank("g")
                mmg(ps[b][:], [(wslot_view[:, kc, c * 128:(c + 1) * 128], xT[:, kc, :]) for kc in range(KC)],
                    [src_slot_key] + xT_keys, [kps(b)])
                vop("dve", "tensor_copy", [kps(b)], [kL(4 + c)], L(4 + c), ps[b][:])
                if stop == "kv3a1":
                    continue
                vop("dve", "tensor_tensor", [kL(4 + c)], [kL(8)], L(8), L(4 + c), L(4 + c), ALU.mult)
                if stop == "kv3a2":
                    continue
                hi = Lb[:, 9, :].bitcast(BF16)[:, 0:512]
                lo = Lb[:, 9, :].bitcast(BF16)[:, 512:1024]
                vop("dve", "tensor_copy", [kL(8)], [kL(9)], hi, L(8))
                vop("dve", "tensor_tensor", [kL(8), kL(9)], [kL(9)], lo, L(8), hi, ALU.subtract)
                if stop == "kv3a3":
                    continue

                def fn(e, c=c, bss=bss, hi=hi, lo=lo):
                    e.matmul(ps[bss][:], onesb[:], hi, start=(c == 0), stop=False)
                    return e.matmul(ps[bss][:], onesb[:], lo, start=False, stop=(c == 3))
                sc.add("pe", fn, reads=["onesb", kL(9)], writes=[kps(bss)])
            if stop in ("kv3a1", "kv3a2", "kv3a3", "kv3a4"):
                return
            vop("dve", "tensor_scalar", [kps(bss)], [kL(10)], L(10), ps[bss][:], 1.0 / 512.0, RMS_EPS, ALU.mult, ALU.add)
            act(L(10), L(10), AF.Sqrt, [kL(10)], [kL(10)])
            vop("dve", "reciprocal", [kL(10)], [kL(11)], L(11), L(10))
            for c in range(4):
                vop("dve", "scalar_tensor_tensor", [kL(4 + c), kL(11), "cst"], [kL(outL + c // 2)],
                    Lb[:, outL + c // 2, :].bitcast(BF16)[:, (c % 2) * 512:(c % 2) * 512 + 512], L(4 + c),
                    cst[:, gcol + c:gcol + c + 1], L(11), ALU.mult, ALU.mult)

        def nrm(outL, c):
            return Lb[:, outL + c // 2, :].bitcast(BF16)[:, (c % 2) * 512:(c % 2) * 512 + 512]

        def rope_tables(ti):
            def fn(e):
                return [e.dma_start(out=L(2, T, I32, 64).rearrange("p (o n) -> p o n", o=1),
                                    in_=pos[0:1, ti * T:(ti + 1) * T].partition_broadcast(64))]
            sc.add("sp", fn, reads=[], writes=[kL(2)], dma="pos", ninc=1)
            A = lambda j: L(j, T, F32, 64)
            vop("dve", "tensor_copy", [kL(2)], [kL(3)], A(3), L(2, T, I32, 64))
            vop("dve", "tensor_scalar", [kL(3), "cst"], [kL(3)], A(3), A(3), invf_ap, None, ALU.mult)
            vop("dve", "tensor_scalar", [kL(3)], [kL(0)], A(0), A(3), 1.0 / (2 * PI), None, ALU.mult)
            vop("dve", "tensor_copy", [kL(0)], [kL(2)], L(2, T, I32, 64), A(0))
            vop("dve", "tensor_copy", [kL(2)], [kL(0)], A(0), L(2, T, I32, 64))
            C1 = 6.28125
            C2_ = float(2 * np.pi - 6.28125)
            vop("dve", "scalar_tensor_tensor", [kL(0), kL(3)], [kL(3)], A(3), A(0), -C1, A(3), ALU.mult, ALU.add)
            vop("dve", "scalar_tensor_tensor", [kL(0), kL(3)], [kL(3)], A(3), A(0), -C2_, A(3), ALU.mult, ALU.add)
            vop("dve", "tensor_scalar", [kL(3)], [kL(0)], A(0), A(3), PI, -2 * PI, ALU.is_gt, ALU.mult)
            vop("dve", "tensor_tensor", [kL(0), kL(3)], [kL(3)], A(3), A(3), A(0), ALU.add)
            vop("dve", "tensor_scalar", [kL(3)], [kL(0)], A(0), A(3), -PI, 2 * PI, ALU.is_lt, ALU.mult)
            vop("dve", "tensor_tensor", [kL(0), kL(3)], [kL(3)], A(3), A(3), A(0), ALU.add)
            vop("dve", "tensor_scalar", [kL(3)], [kL(3)], A(3), A(3), PI, -PI, ALU.min, ALU.max)
            act(A(1), A(3), AF.Sin, [kL(3)], [kL(1)])
            vop("dve", "tensor_scalar", [kL(1), "cst"], [kL(1)], A(1), A(1), sgn_ap, None, ALU.mult)
            vop("dve", "scalar_tensor_tensor", [kL(3)], [kL(3)], A(3), A(3), -1.0, A(3), ALU.mult, ALU.max)
            vop("dve", "tensor_scalar", [kL(3)], [kL(3)], A(3), A(3), -1.0, PI / 2, ALU.mult, ALU.add)
            act(A(0), A(3), AF.Sin, [kL(3)], [kL(0)])

        def rope_apply(pa, pb, out_ap, scale, reads, writes):
            vop("dve", "tensor_tensor", [kps(pa), kL(0)], [kL(12)], L(12, T, F32, 64), ps[pa][0:64, :], L(0, T, F32, 64), ALU.mult)
            vop("dve", "tensor_tensor", [kps(pb), kL(1)], [kL(13)], L(13, T, F32, 64), ps[pb][0:64, :], L(1, T, F32, 64), ALU.mult)
            vop("dve", "scalar_tensor_tensor", [kL(12), kL(13)] + reads, writes, out_ap, L(12, T, F32, 64), scale,
                L(13, T, F32, 64), ALU.mult, ALU.add)

        def kv_path(ti):
            sKV, wKV = ring_load([(lambda sl: sl[:].rearrange("p (k f) -> p k f", f=512),
                                   wb["win"][:, 2560:3072].rearrange("(k p) f -> p k f", p=128))], reads=wkeys("win"))
            rms_feat(0, C_KVG, ("w", sKV), wKV[:].rearrange("p (k f) -> p k f", f=512), 12)
            sW, wW = ring_load([(lambda sl: sl[:].rearrange("p (k f) -> p k f", f=2048),
                                 wb["wkv"].rearrange("(k p) f -> p k f", p=128))], reads=wkeys("wkv"))
            wv = wW[:].rearrange("p (k f) -> p k f", f=2048)
            nkeys = [kL(12), kL(13)]
            if stop and stop.startswith("kv3a"):
                return
            for h in range(8):
                b = bank("u")
                mmg(ps[b][:], [(wv[:, kc, h * 128:(h + 1) * 128], nrm(12, kc)) for kc in range(4)],
                    [("w", sW)] + nkeys, [kps(b)])
                evac_copy(r2[:, h, :], ps[b][:], [kps(b)], [kr2(h)])

            def fnk(e):
                return [e.dma_start(out=KTs[:, :, ti * T:(ti + 1) * T].rearrange("h d t -> d h t"), in_=r2[:, 0:8, :])]
            sc.add("act", fnk, reads=[kr2(h) for h in range(8)], writes=[("KT", ti)], dma="stK", ninc=1)
            if stop == "kv3b":
                return
            vst = r2[:, 8:16, :].rearrange("p (s a) t -> p s (a t)", s=4)
            for s in range(4):
                for hf in range(2):
                    b = bank("d")
                    mmg(ps[b][:], [(nrm(12, kc)[:, s * 128:(s + 1) * 128], wv[:, kc, 1024 + hf * 512:1024 + (hf + 1) * 512])
                                   for kc in range(4)], [("w", sW)] + nkeys, [kps(b)])
                    evac_copy(vst[:, s, hf * 512:(hf + 1) * 512], ps[b][:], [kps(b)], [kr2(8 + 2 * s + hf)])

            def fnv(e):
                return [e.dma_start(out=Vs[:, :, ti * 4 + s, :].rearrange("h p d -> p h d"),
                                    in_=vst[:, s, :].rearrange("p (h d) -> p h d", d=128)) for s in range(4)]
            sc.add("act", fnv, reads=[kr2(g) for g in range(8, 16)], writes=[("V", ti)], dma="stV", ninc=4)

        def kpe_path(ti):
            sP, wP = ring_load([(lambda sl: sl[:, 0:2048].rearrange("p (k f) -> p k f", f=128),
                                 wb["win"][:, 3072:3200].rearrange("(k p) f -> p k f", p=128))], reads=wkeys("win"))
            wpv = wP[:, 0:2048].rearrange("p (k f) -> p k f", f=128)
            pa, pb = bank("g"), bank("u")
            mmg(ps[pa][0:64, :], [(wpv[:, kc, 0:64], xT[:, kc, :]) for kc in range(KC)], [("w", sP)] + xT_keys, [kps(pa)])
            mmg(ps[pb][0:64, :], [(wpv[:, kc, 64:128], xT[:, kc, :]) for kc in range(KC)], [("w", sP)] + xT_keys, [kps(pb)])
            rope_apply(pa, pb, L(11, T, BF16, 64), 1.0, [], [kL(11)])

            def fnp(e):
                return [e.dma_start(out=KPEs[:, ti * T:(ti + 1) * T], in_=L(11, T, BF16, 64))]
            sc.add("act", fnp, reads=[kL(11)], writes=[("KPE", ti)], dma="stP", ninc=1)

        def q_path():
            sQ, wQ = ring_load([(lambda sl: sl[:].rearrange("p (k f) -> p k f", f=512),
                                 wb["win"][:, 2048:2560].rearrange("(k p) f -> p k f", p=128))], reads=wkeys("win"))
            rms_feat(0, C_QG, ("w", sQ), wQ[:].rearrange("p (k f) -> p k f", f=512), 12)
            sW, wW = ring_load([(lambda sl: sl[:].rearrange("p (k f) -> p k f", f=2048),
                                 wb["wq"].rearrange("(k p) f -> p k f", p=128))], reads=wkeys("wq"))
            wv = wW[:].rearrange("p (k f) -> p k f", f=2048)
            nkeys = [kL(12), kL(13)]
            for h in range(8):
                b = bank("d")
                mmg(ps[b][:], [(wv[:, kc, h * 256:h * 256 + 128], nrm(12, kc)) for kc in range(4)],
                    [("w", sW)] + nkeys, [kps(b)])
                act(r2[:, 16 + h, :], ps[b][:], AF.Copy, [kps(b)], [kr2(16 + h)], scale=QSCALE)
                pa, pb = bank("g"), bank("u")
                mmg(ps[pa][0:64, :], [(wv[:, kc, h * 256 + 128:h * 256 + 192], nrm(12, kc)) for kc in range(4)],
                    [("w", sW)] + nkeys, [kps(pa)])
                mmg(ps[pb][0:64, :], [(wv[:, kc, h * 256 + 192:h * 256 + 256], nrm(12, kc)) for kc in range(4)],
                    [("w", sW)] + nkeys, [kps(pb)])
                vop("dve", "tensor_tensor", [kps(pa), kL(0)], [kL(8)], L(8, T, F32, 64), ps[pa][0:64, :], L(0, T, F32, 64), ALU.mult)
                vop("dve", "tensor_tensor", [kps(pb), kL(1)], [kL(9)], L(9, T, F32, 64), ps[pb][0:64, :], L(1, T, F32, 64), ALU.mult)
                vop("dve", "tensor_tensor", [kL(8), kL(9)], [kL(8)], L(8, T, F32, 64), L(8, T, F32, 64), L(9, T, F32, 64), ALU.add)
                vop("dve", "tensor_scalar", [kL(8)], [kr2(24 + h)], r2[0:64, 24 + h, :], L(8, T, F32, 64), QSCALE, None, ALU.mult)

        def lru(ti, own):
            first_own = (ti == nth)
            slots = {}
            for half in range(2):
                sA, wA = ring_load([(lambda sl: sl[:].rearrange("p (k f) -> p k f", f=512),
                                     wb["win"][:, half * 512:(half + 1) * 512].rearrange("(k p) f -> p k f", p=128))],
                                   reads=wkeys("win"))
                slots[("a", half)] = (sA, wA[:].rearrange("p (k f) -> p k f", f=512))
                if own:
                    sG, wG = ring_load([(lambda sl: sl[:].rearrange("p (k f) -> p k f", f=512),
                                         wb["win"][:, 1024 + half * 512:1024 + (half + 1) * 512].rearrange("(k p) f -> p k f", p=128))],
                                       reads=wkeys("win"))
                    slots[("g", half)] = (sG, wG[:].rearrange("p (k f) -> p k f", f=512))
                for cc in range(4):
                    c = half * 4 + cc
                    sA, wAv = slots[("a", half)]
                    b = bank("g")
                    mmg(ps[b][:], [(wAv[:, kc, cc * 128:(cc + 1) * 128], xT[:, kc, :]) for kc in range(KC)],
                        [("w", sA)] + xT_keys, [kps(b)])
                    lin = Lb[:, 2, 0:515]
                    if first_own:
                        vop("dve", "tensor_scalar", ["halo", "cst"], [kL(2)], lin[:, 0:3], halo[:, c, 0:3], flag_ap, None, ALU.mult)
                        vop("dve", "tensor_scalar", ["carry", "cst"], ["carry"], carry[:, c:c + 1], carry[:, c:c + 1], flag_ap, None, ALU.mult)
                    else:
                        vop("dve", "tensor_copy", ["halo"], [kL(2)], lin[:, 0:3], halo[:, c, 0:3])
                    act(lin[:, 3:515], ps[b][:], AF.Copy, [kps(b)], [kL(2)])
                    vop("dve", "tensor_copy", [kL(2)], ["halo"], halo[:, c, 0:3], lin[:, 512:515])
                    act(L(3), lin[:, 3:515], AF.Identity, [kL(2), "cst"], [kL(3)], bias=cv[:, c, 4:5], scale=cv[:, c, 3:4])
                    for k in range(3):
                        vop("dve", "scalar_tensor_tensor", [kL(2), kL(3), "cst"], [kL(3)], L(3), lin[:, k:k + 512],
                            cv[:, c, k:k + 1], L(3), ALU.mult, ALU.add)
                    act(L(4, T, BF16), L(3), AF.Copy, [kL(3)], [kL(4)])
                    ba, bx = bank("u"), bank("d")
                    mmg(ps[ba][:], [(wabd[:, c, :], L(4, T, BF16))], ["wabd", kL(4)], [kps(ba)])
                    mmg(ps[bx][:], [(wxbd[:, c, :], L(4, T, BF16))], ["wxbd", kL(4)], [kps(bx)])
                    act(L(5), ps[ba][:], AF.Sigmoid, [kps(ba), "cst"], [kL(5)], bias=cv[:, c, 5:6])
                    act(L(6), ps[bx][:], AF.Sigmoid, [kps(bx), "cst"], [kL(6)], bias=cv[:, c, 6:7])
                    act(L(7), L(5), AF.Exp, [kL(5), "cl"], [kL(7)], scale=clt[:, 1, c:c + 1])
                    act(L(5), L(5), AF.Exp, [kL(5), "cl"], [kL(5)], scale=clt[:, 2, c:c + 1])
                    act(L(5), L(5), AF.Sqrt, [kL(5)], [kL(5)], bias=1.0, scale=-1.0)
                    vop("dve", "tensor_tensor", [kL(6), kL(3)], [kL(6)], L(6), L(6), L(3), ALU.mult)
                    vop("dve", "tensor_tensor", [kL(6), kL(5)], [kL(6)], L(6), L(6), L(5), ALU.mult)
                    vop("dve", "tensor_tensor_scan", [kL(7), kL(6), "carry"], [kL(8)], L(8), L(7), L(6),
                        carry[:, c:c + 1], ALU.mult, ALU.add)
                    vop("dve", "tensor_copy", [kL(8)], ["carry"], carry[:, c:c + 1], L(8)[:, T - 1:T])
                    if own:
                        sG, wGv = slots[("g", half)]
                        bq = bank("m")
                        mmg(ps[bq][:], [(wGv[:, kc, cc * 128:(cc + 1) * 128], xT[:, kc, :]) for kc in range(KC)],
                            [("w", sG)] + xT_keys, [kps(bq)])
                        act(L(9), ps[bq][:], AF.Copy, [kps(bq)], [kL(9)])
                        vop("dve", "tensor_tensor", [kL(9)], [kL(10)], L(10), L(9), L(9), ALU.mult)
                        vop("dve", "tensor_scalar", [kL(10)], [kL(10)], L(10), L(10), 0.044715, 1.0, ALU.mult, ALU.add)
                        vop("dve", "tensor_tensor", [kL(10), kL(9)], [kL(10)], L(10), L(10), L(9), ALU.mult)
                        act(L(10), L(10), AF.Sigmoid, [kL(10)], [kL(10)], scale=1.5957691216057308)
                        vop("dve", "tensor_tensor", [kL(10), kL(9)], [kL(10)], L(10), L(10), L(9), ALU.mult)
                        vop("dve", "tensor_tensor", [kL(10), kL(8)], [kr2(c)], r2[:, c, :], L(10), L(8), ALU.mult)

        def out_proj():
            for dc in range(4):
                sW, wW = ring_load([(lambda sl: sl[:].rearrange("p (k f) -> p k f", f=512),
                                     wb["wout"][:, dc * 512:(dc + 1) * 512].rearrange("(k p) f -> p k f", p=128))],
                                   reads=wkeys("wout"))
                wv = wW[:].rearrange("p (k f) -> p k f", f=512)
                for s in range(4):
                    b = bank("d")
                    mmg(ps[b][:], [(r2[:, kc, s * 128:(s + 1) * 128], wv[:, kc, :]) for kc in range(KC)],
                        [("w", sW)] + [kr2(g) for g in range(16)], [kps(b)])
                    vop("dve", "scalar_tensor_tensor", [kps(b), ("xtm", s, dc)], [("xtm", s, dc)],
                        xtm[:, s, dc * 512:(dc + 1) * 512], xtm[:, s, dc * 512:(dc + 1) * 512], ALPHA, ps[b][:],
                        ALU.mult, ALU.add)

        def ple(to):
            def fnp(e):
                return [e.dma_start(out=Lb[:, a, 0:512].rearrange("p (s c) -> p s c", c=256),
                                    in_=ps_in[to * T + a * 256:to * T + (a + 1) * 256, :].rearrange("(s p) c -> p s c", p=128))
                        for a in range(2)]
            sc.add("sp", fnp, reads=[], writes=[kL(0), kL(1)], dma="p", ninc=2)

            class _PV:
                def __getitem__(self, idx):
                    _, s, cs = idx
                    return Lb[:, s // 2, (s % 2) * 256 + cs.start:(s % 2) * 256 + cs.stop]
            ptv = _PV()
            pT = Lb[:, 2, :].bitcast(BF16)[:, 0:1024].rearrange("p (k t) -> p k t", t=512)
            for k2 in range(2):
                b = bank("m")

                def fn(e, k2=k2, b=b):
                    inst = None
                    for s in range(4):
                        inst = e.transpose(ps[b][:, s * 128:(s + 1) * 128], ptv[:, s, k2 * 128:(k2 + 1) * 128], ident32)
                    return inst
                sc.add("pe", fn, reads=[kL(0), kL(1), "cst"], writes=[kps(b)])
                vop("dve", "tensor_copy", [kps(b)], [kL(2)], pT[:, k2, :], ps[b][:])
            if stop == "ple1":
                return
            sB, wB_ = ring_load([(lambda sl: sl[:, 0:4096].rearrange("p (k f) -> p k f", f=2048),
                                  wb["wpp"].rearrange("(k p) f -> p k f", p=128)),
                                 (lambda sl: sl[:].bitcast(F32)[:, 2048:4096].rearrange("p (o n) -> p o n", o=1),
                                  bg_in[0:1, :].partition_broadcast(128))], reads=wkeys("wpp"))
            wpp = wB_[:, 0:4096].rearrange("p (k f) -> p k f", f=2048)
            bgB = wB_[:].bitcast(F32)[:, 2048:4096]
            if stop == "ple2":
                vop("dve", "tensor_copy", [("w", sB)], [kL(3)], L(3), bgB[:, 0:512])
                return
            for dc in range(1 if stop == "ple3" else 4):
                sW, wW = ring_load([(lambda sl: sl[:].rearrange("p (k f) -> p k f", f=512),
                                     wb["wpg"][:, dc * 512:(dc + 1) * 512].rearrange("(k p) f -> p k f", p=128))],
                                   reads=wkeys("wpg"))
                wv = wW[:].rearrange("p (k f) -> p k f", f=512)
                for s in range(4):
                    bgk, bpk = bank("g"), bank("u")
                    mmg(ps[bgk][:], [(xT[:, kc, s * 128:(s + 1) * 128], wv[:, kc, :]) for kc in range(KC)],
                        [("w", sW)] + xT_keys, [kps(bgk)])
                    mmg(ps[bpk][:], [(pT[:, k2, s * 128:(s + 1) * 128], wpp[:, k2, dc * 512:(dc + 1) * 512]) for k2 in range(2)],
                        [("w", sB), kL(2)], [kps(bpk)])
                    lj = 3 + (s % 2)
                    vop("dve", "tensor_tensor", [kps(bgk), ("w", sB)], [kL(lj)], L(lj), ps[bgk][:], bgB[:, dc * 512:(dc + 1) * 512], ALU.add)
                    act(L(lj), L(lj), AF.Sigmoid, [kL(lj)], [kL(lj)])
                    vop("dve", "tensor_tensor", [kL(lj), kps(bpk)], [kL(lj)], L(lj), L(lj), ps[bpk][:], ALU.mult)
                    vop("dve", "scalar_tensor_tensor", [kL(lj), ("xtm", s, dc)], [("xtm", s, dc)],
                        xtm[:, s, dc * 512:(dc + 1) * 512], xtm[:, s, dc * 512:(dc + 1) * 512], ALPHA, L(lj),
                        ALU.mult, ALU.add)

        def store_out(to):
            def fn(e):
                return [e.dma_start(out=yout[to * T:(to + 1) * T, :].rearrange("(s p) d -> p s d", p=128), in_=xtm[:])]
            sc.add("act", fn, reads=xtm_keys, writes=[("y", to)], dma="sty", ninc=1)

        def attention_full(ti):
            nkc = ti + 1
            ystage = [Lb[:, 4, :].bitcast(BF16)[:, 0:1024], Lb[:, 5, :].bitcast(BF16)[:, 0:1024]]
            for h in range(8):
                nk = nkc * T
                sK, wKt = ring_load([(lambda sl, nk=nk: sl[:, 0:nk], KTs[h, :, 0:nk])], reads=[("KT", j) for j in range(nkc)])
                sP, wPt = ring_load([(lambda sl, nk=nk: sl[0:64, 0:nk], KPEs[:, 0:nk])], reads=[("KPE", j) for j in range(nkc)])
                sV, wVt = ring_load([(lambda sl, nkc=nkc: sl[:, 0:nkc * 512].rearrange("p (b d) -> p b d", d=128),
                                      Vs[h, :, 0:nkc * 4, :])], reads=[("V", j) for j in range(nkc)])
                wK, wP = wKt, wPt
                wVv = wVt[:].rearrange("p (b d) -> p b d", d=128)
                for qb in range(4):
                    qs = slice(qb * 128, (qb + 1) * 128)
                    qn = r2[:, 16 + h, qs]
                    qp = r2[0:64, 24 + h, qs]
                    ndiag = (qb + 1) * 128

                    def scores(b, kcx, n, qn=qn, qp=qp, wK=wK, wP=wP, sK=sK, sP=sP, h=h):
                        mmg(ps[b][:, 0:n], [(qn, wK[:, kcx * T:kcx * T + n]), (qp, wP[0:64, kcx * T:kcx * T + n])],
                            [("w", sK), ("w", sP), kr2(16 + h), kr2(24 + h)], [kps(b)])
                    for kcx in range(nkc):
                        diag = (kcx == ti)
                        n = ndiag if diag else T
                        b = bank("g")
                        scores(b, kcx, n)
                        if diag:
                            vop("dve", "tensor_tensor", [kps(b), "cst"], [kL(6)], L(6)[:, 0:n], ps[b][:, 0:n],
                                cm3[:, (3 - qb) * 128:(3 - qb) * 128 + n], ALU.add)
                            vop("dve", "reduce_max", [kL(6)], ["mxt"], mxt[:, kcx:kcx + 1], L(6)[:, 0:n], AX.X)
                        else:
                            vop("dve", "reduce_max", [kps(b)], ["mxt"], mxt[:, kcx:kcx + 1], ps[b][:, 0:n], AX.X)
                    vop("dve", "tensor_scalar", ["mxt", "cst"], ["mxt"], mxt[:, 0:nth], mxt[:, 0:nth], mbias_ap, None, ALU.add)
                    vop("dve", "reduce_max", ["mxt"], ["mxm"], sm[:, 8:9], mxt[:, 0:nkc], AX.X)
                    vop("dve", "tensor_scalar", ["mxm"], ["negm"], sm[:, 9:10], sm[:, 8:9], -1.0, None, ALU.mult)
                    vop("dve", "tensor_scalar", ["negm", "cst"], ["negmo"], sm[:, 10:11], sm[:, 9:10], mbias_ap, None, ALU.add)
                    bo = bank("d")
                    nblk_tot = (nkc - 1) * 4 + (qb + 1)
                    blk_i = 0
                    for kcx in range(nkc):
                        diag = (kcx == ti)
                        n = ndiag if diag else T
                        pj = 7 + (kcx % 2)
                        Pt = L(pj, T, BF16)
                        rs = mxt[:, 20 + (kcx % 20):21 + (kcx % 20)]
                        if diag:
                            act(Pt[:, 0:n], L(6)[:, 0:n], AF.Exp, [kL(6), "negm"], [kL(pj), ("rs", kcx)], bias=sm[:, 9:10], accum_out=rs)
                        else:
                            b = bank("g")
                            scores(b, kcx, n)
                            act(Pt[:, 0:n], ps[b][:, 0:n], AF.Exp, [kps(b), "negm", "negmo"], [kL(pj), ("rs", kcx)],
                                bias=(sm[:, 10:11] if kcx < nth else sm[:, 9:10]), accum_out=rs)
                        nb = n // 128
                        bt = bank("u")
                        ptp = ps[bt][:].bitcast(BF16)

                        def fnt(e, Pt=Pt, nb=nb, ptp=ptp):
                            inst = None
                            for jb in range(nb):
                                inst = e.transpose(ptp[:, jb * 128:(jb + 1) * 128], Pt[:, jb * 128:(jb + 1) * 128], identb[:])
                            return inst
                        sc.add("pe", fnt, reads=[kL(pj), "identb"], writes=[kps(bt)])
                        tj = 9 + (kcx % 2)
                        PT = L(tj, T, BF16)
                        evac_copy(PT[:, 0:n], ptp[:, 0:n], [kps(bt)], [kL(tj)])

                        def fnpv(e, PT=PT, nb=nb, kcx=kcx, blk_i=blk_i, bo=bo, nblk_tot=nblk_tot, wVv=wVv):
                            inst = None
                            for jb in range(nb):
                                gi = blk_i + jb
                                inst = e.matmul(ps[bo][:, 0:128], PT[:, jb * 128:(jb + 1) * 128], wVv[:, kcx * 4 + jb, :],
                                                start=(gi == 0), stop=(gi == nblk_tot - 1))
                            return inst
                        sc.add("pe", fnpv, reads=[kL(tj), ("w", sV)], writes=[kps(bo)])
                        blk_i += nb
                    vop("dve", "reduce_sum", [("rs", k) for k in range(nkc)], ["lsum"], sm[:, 11:12], mxt[:, 20:20 + nkc], AX.X)
                    vop("dve", "reciprocal", ["lsum"], ["linv"], sm[:, 12:13], sm[:, 11:12])
                    yj = 11 + (qb % 2)
                    vop("dve", "tensor_scalar", [kps(bo), "linv"], [kL(yj)], L(yj, 128, BF16), ps[bo][:, 0:128], sm[:, 12:13], None, ALU.mult)
                    bt = bank("u")
                    ptp = ps[bt][:].bitcast(BF16)

                    def fny(e, yj=yj, ptp=ptp):
                        return e.transpose(ptp[:, 0:128], L(yj, 128, BF16), identb[:])
                    sc.add("pe", fny, reads=[kL(yj), "identb"], writes=[kps(bt)])
                    evac_copy(r2[:, 8 + h, qs], ptp[:, 0:128], [kps(bt)], [kr2(8 + h)])

        order = ["x", "tr", "ffn", "ln1", "kv", "lru", "attn", "ln2", "ffn2", "full"]
        lvl = (4 if stop.startswith("kv") else 9 if stop.startswith("ple") else order.index(stop)) if stop else len(order) - 1

        def tile_prog(ti):
            own = ti >= nth
            to = ti - nth
            load_x(ti)
            if lvl < 1:
                return
            transposes()
            prescale()
            if lvl < 2:
                return
            ffn("w1g", "w1u", "w1d")
            if lvl < 3:
                return
            layernorm(0)
            if lvl < 4:
                return
            transposes()
            if stop == "kv0":
                return
            rope_tables(ti)
            if stop == "kv1":
                return
            kpe_path(ti)
            if stop == "kv2":
                return
            kv_path(ti)
            if stop and stop.startswith("kv3"):
                return
            if own:
                q_path()
            if lvl < 5:
                return
            lru(ti, own)
            if not own or lvl < 6:
                return
            attention_full(ti)
            if lvl < 7:
                return
            out_proj()
            layernorm(1)
            if lvl < 8:
                return
            transposes()
            prescale()
            ffn("w2g", "w2u", "w2d")
            layernorm(2)
            if lvl < 9:
                return
            transposes()
            ple(to)
            if stop and stop.startswith("ple"):
                return
            layernorm(3)

        for ti in range(NTILE):
            tile_prog(ti)
            if ti >= nth:
                store_out(ti - nth)
        sc.add("sp", None, reads=[("y", t_) for t_ in range(nth)], writes=[])
        sc.emit(nc, stack)
    return nc


def _host_layouts(inp, S):
    f = lambda a: np.ascontiguousarray(np.asarray(a, dtype=np.float32))
    w = {}
    w["w1g"], w["w1u"], w["w1d"] = f(inp["ffn1_w_gate"][0]), f(inp["ffn1_w_up"][0]), f(inp["ffn1_w_down"][0])
    w["w2g"], w["w2u"], w["w2d"] = f(inp["ffn2_w_gate"][0]), f(inp["ffn2_w_up"][0]), f(inp["ffn2_w_down"][0])
    win = f(inp["w_in"][0])
    kpe = win[:, 3072:3136]
    w["win"] = np.ascontiguousarray(np.concatenate([win, kpe[:, 32:64], kpe[:, 0:32]], axis=1))
    wq = f(inp["w_q_up"][0]).reshape(512, 8, 192)
    w["wq"] = np.ascontiguousarray(np.concatenate([wq, wq[:, :, 160:192], wq[:, :, 128:160]], axis=2).reshape(512, 2048))
    wkv = f(inp["w_kv_up"][0]).reshape(512, 8, 256)
    w["wkv"] = np.ascontiguousarray(np.concatenate([wkv[:, :, 0:128].reshape(512, 1024), wkv[:, :, 128:256].reshape(512, 1024)], axis=1))
    w["wout"], w["wpg"], w["wpp"] = f(inp["w_out"][0]), f(inp["ple_w_gate"][0]), f(inp["ple_w_proj"][0])
    lnp = np.stack([np.concatenate([f(inp[f"ln{i}_g"][0]), f(inp[f"ln{i}_b"][0])]) for i in (1, 2, 3, 4)])
    bg = f(inp["ple_b_gate"][0]).reshape(1, 2048)
    cst = np.zeros((128, NCST), np.float32)
    cvv = np.zeros((128, 8, 8), np.float32)
    chan = lambda v: f(v).reshape(8, 128).T
    cw = f(inp["conv_w"][0])
    for k in range(4):
        cvv[:, :, k] = chan(cw[k])
    cvv[:, :, 4] = chan(inp["conv_b"][0])
    cvv[:, :, 5] = chan(f(inp["lru_b_a"][0]).reshape(-1))
    cvv[:, :, 6] = chan(f(inp["lru_b_x"][0]).reshape(-1))
    cvv[:, :, 7] = chan(inp["lru_lambda"][0])
    cst[:, C_CV:C_CV + 64] = cvv.reshape(128, 64)
    cst[:, C_QG:C_QG + 4] = f(inp["q_norm_g"][0]).reshape(4, 128).T
    cst[:, C_KVG:C_KVG + 4] = f(inp["kv_norm_g"][0]).reshape(4, 128).T
    invf = (10000.0 ** (-np.arange(0, 64, 2, dtype=np.float32) / np.float32(64))).astype(np.float32)
    cst[0:64, C_ROPE] = np.tile(invf, 2)
    cst[0:64, C_ROPE + 1] = np.concatenate([-np.ones(32, np.float32), np.ones(32, np.float32)])
    cst[:, C_ID:C_ID + 128] = np.eye(128, dtype=np.float32)
    qi = np.arange(128)[:, None]
    kj = np.arange(512)[None, :]
    cst[:, C_CM:C_CM + 512] = np.where(kj <= qi + 384, 0.0, NEG).astype(np.float32)
    wbd = np.zeros((128, 2, 8, 128), np.float32)
    wa, wx = f(inp["lru_w_a"][0]), f(inp["lru_w_x"][0])
    for c in range(8):
        for hb in range(2):
            wbd[hb * 64:(hb + 1) * 64, 0, c, hb * 64:(hb + 1) * 64] = wa[2 * c + hb]
            wbd[hb * 64:(hb + 1) * 64, 1, c, hb * 64:(hb + 1) * 64] = wx[2 * c + hb]
    shared = dict(w, lnp=lnp, bg=bg, wbd=wbd.reshape(128, 2048))
    x = f(inp["x"])
    p = f(inp["p"][0])
    posn = np.asarray(inp["positions"]).astype(np.int32)
    S2 = S // 2
    maps = []
    for c in range(8):
        b, h = c // 2, c % 2
        own = slice(h * S2, (h + 1) * S2)
        oth = slice((1 - h) * S2, (2 - h) * S2)
        cc = cst.copy()
        cc[:, C_FLAG] = 1.0 if h == 1 else 0.0
        cc[:, C_FLAG + 1] = 0.0 if h == 1 else NEG
        m = dict(shared)
        m["xs"] = np.ascontiguousarray(np.concatenate([x[b, oth], x[b, own]], axis=0))
        m["ps"] = np.ascontiguousarray(p[b, own])
        m["pos"] = np.ascontiguousarray(np.concatenate([posn[b, oth], posn[b, own]])[None, :])
        m["cst"] = cc
        maps.append(m)
    return maps


_NC_CACHE = {}


def kernel(**inputs):
    x = np.asarray(inputs["x"])
    B, S, _ = x.shape
    nth = S // 2 // T
    inputs = dict(inputs)
    stop = inputs.pop("_stop", None)
    key = (nth, stop)
    if key not in _NC_CACHE:
        _NC_CACHE[key] = build(nth, stop)
    nc = _NC_CACHE[key]
    ncores = inputs.pop("_ncores", 8)
    maps = _host_layouts(inputs, S)[:ncores]
    res = run_bass_kernel_spmd(nc, maps, core_ids=list(range(ncores)))
    out = np.zeros((B, S, D), np.float32)
    S2 = S // 2
    for c in range(ncores):
        b, h = c // 2, c % 2
        out[b, h * S2:(h + 1) * S2] = np.asarray(res.results[c]["y"], dtype=np.float32).reshape(S2, D)
    return out
```

```python
import numpy as np
import concourse.bass as bass
import concourse.mybir as mybir
from concourse.bass_utils import run_bass_kernel_spmd

F32 = mybir.dt.float32
BF16 = mybir.dt.bfloat16
I32 = mybir.dt.int32
AF = mybir.ActivationFunctionType
ALU = mybir.AluOpType
AX = mybir.AxisListType

T = 512
D = 2048
KC = 16
DFF = 5632
NFG = 11
ALPHA = 2.0 ** 0.25
LN_EPS = 1e-5
RMS_EPS = 1e-6
QSCALE = 192.0 ** -0.5
NSLOT = 5
LSZ = 528
NL = 14
NEG = -30000.0
PI = float(np.pi)

C_CV = 0
C_QG = 64
C_KVG = 68
C_FLAG = 72
C_ROPE = 74
C_ID = 76
C_CM = 204
NCST = 716


class _Op:
    __slots__ = ("eng", "fn", "deps", "signal", "sigval", "dma", "ninc")


class Sched:
    def __init__(self):
        self.ops = []
        self.lastw = {}
        self.readers = {}

    def add(self, eng, fn, reads=(), writes=(), dma=None, ninc=1):
        idx = len(self.ops)
        deps = {}

        def dep(j):
            o = self.ops[j]
            if o.dma is None and o.eng == "pe" and eng == "pe" and dma is None:
                return
            k = ("d", o.dma) if o.dma is not None else ("e", o.eng)
            if k not in deps or deps[k] < j:
                deps[k] = j

        for k in reads:
            w = self.lastw.get(k)
            if w is not None:
                dep(w)
        for k in writes:
            w = self.lastw.get(k)
            if w is not None:
                dep(w)
            for r in self.readers.get(k, ()):
                dep(r)
        for k in writes:
            self.lastw[k] = idx
            self.readers[k] = []
        for k in reads:
            if k not in writes:
                self.readers.setdefault(k, []).append(idx)
        op = _Op()
        op.eng, op.fn, op.dma, op.ninc = eng, fn, dma, ninc
        op.deps = list(deps.values())
        op.signal = False
        op.sigval = 0
        for j in op.deps:
            self.ops[j].signal = True
        self.ops.append(op)
        return idx

    def emit(self, nc, stack):
        engs = ("pe", "act", "dve", "pool", "sp")
        esem = {e: stack.enter_context(nc.semaphore("s_" + e)) for e in engs}
        dsem = {}
        ecnt = {e: 0 for e in engs}
        dcnt = {}
        for op in self.ops:
            if op.dma is not None:
                if op.dma not in dsem:
                    dsem[op.dma] = stack.enter_context(nc.semaphore("d_" + op.dma))
                    dcnt[op.dma] = 0
                dcnt[op.dma] += 16 * op.ninc
                op.sigval = dcnt[op.dma]
            elif op.signal:
                ecnt[op.eng] += 1
                op.sigval = ecnt[op.eng]
        ops = self.ops
        block = stack.enter_context(nc.Block())

        def run(e, name):
            waited = {}
            for op in ops:
                if op.eng != name:
                    continue
                for j in op.deps:
                    p = ops[j]
                    sem = dsem[p.dma] if p.dma is not None else esem[p.eng]
                    key = id(sem)
                    if waited.get(key, 0) < p.sigval:
                        e.wait_ge(sem, p.sigval)
                        waited[key] = p.sigval
                if op.fn is None:
                    continue
                r = op.fn(e)
                if op.dma is not None:
                    for inst in r:
                        inst.then_inc(dsem[op.dma], 16)
                elif op.signal:
                    inst = r[-1] if isinstance(r, (list, tuple)) else r
                    inst.then_inc(esem[name], 1)

        @block.tensor
        def _(e):
            run(e, "pe")

        @block.scalar
        def _(e):
            run(e, "act")

        @block.vector
        def _(e):
            run(e, "dve")

        @block.gpsimd
        def _(e):
            run(e, "pool")

        @block.sync
        def _(e):
            run(e, "sp")


def build(nth, stop=None):
    import contextlib

    nc = bass.Bass("TRN2", target_bir_lowering=False)
    S2 = nth * T
    NTOK = 2 * S2
    NTILE = 2 * nth
    NBLK = NTOK // 128

    def din(name, shape, dt=F32):
        return nc.dram_tensor(name, shape, dt, kind="ExternalInput").ap()

    def dint(name, shape, dt=BF16):
        return nc.dram_tensor(name, shape, dt, kind="Internal").ap()

    xs = din("xs", [NTOK, D])
    ps_in = din("ps", [S2, 256])
    pos = din("pos", [1, NTOK], I32)
    cst_in = din("cst", [128, NCST])
    wbd_in = din("wbd", [128, 2048])
    lnp = din("lnp", [4, 4096])
    bg_in = din("bg", [1, 2048])
    wshapes = dict(w1g=[D, DFF], w1u=[D, DFF], w1d=[DFF, D], win=[D, 3200], wkv=[512, 2048],
                   wq=[512, 2048], wout=[D, D], w2g=[D, DFF], w2u=[D, DFF], w2d=[DFF, D],
                   wpg=[D, D], wpp=[256, D])
    w32 = {k: din(k, v) for k, v in wshapes.items()}
    wb = {k: dint(k + "_b", v) for k, v in wshapes.items()}
    KTs = dint("KTs", [8, 128, NTOK])
    KPEs = dint("KPEs", [64, NTOK])
    Vs = dint("Vs", [8, 128, NBLK, 128])
    yout = nc.dram_tensor("y", [S2, D], F32, kind="ExternalOutput").ap()

    sc = Sched()
    stack = contextlib.ExitStack()
    with stack:
        def sb(name, shape, dt):
            return stack.enter_context(nc.sbuf_tensor(name, shape, dt))

        xtm = sb("xtm", [128, 4, D], F32)
        xT = sb("xT", [128, KC, T], BF16)
        r2 = sb("r2", [128, 32, T], BF16)
        wr = [sb(f"wr{i}", [128, 8192], BF16) for i in range(NSLOT)]
        Lb = sb("Lb", [128, NL, LSZ], F32)
        cst = sb("cst_sb", [128, NCST], F32)
        wabd = sb("wabd", [128, 8, 128], BF16)
        wxbd = sb("wxbd", [128, 8, 128], BF16)
        identb = sb("identb", [128, 128], BF16)
        onesb = sb("onesb", [128, 128], BF16)
        clt = sb("clt", [128, 3, 8], F32)
        halo = sb("halo", [128, 8, 4], F32)
        carry = sb("carry", [128, 8], F32)
        sm = sb("sm", [128, 64], F32)
        bnst4 = sb("bnst4", [128, 4, 24], F32)
        mxt = sb("mxt", [128, 40], F32)
        ps = [stack.enter_context(nc.psum_tensor(f"ps{i}", [128, 512], F32)) for i in range(8)]

        ident32 = cst[:, C_ID:C_ID + 128]
        cm3 = cst[:, C_CM:C_CM + 512]
        cv = cst[:, C_CV:C_CV + 64].rearrange("p (c j) -> p c j", j=8)
        flag_ap = cst[:, C_FLAG:C_FLAG + 1]
        mbias_ap = cst[:, C_FLAG + 1:C_FLAG + 2]
        invf_ap = cst[0:64, C_ROPE:C_ROPE + 1]
        sgn_ap = cst[0:64, C_ROPE + 1:C_ROPE + 2]

        def L(j, n=T, dt=F32, p=128):
            a = Lb[0:p, j, :]
            if dt == BF16:
                return a.bitcast(BF16)[:, 0:n]
            if dt == I32:
                return a.bitcast(I32)[:, 0:n]
            return a[:, 0:n]

        def kL(j):
            return ("L", j)

        bank_ctr = {}

        def bank(group):
            base = dict(g=0, u=2, d=4, m=6, dd=4)[group]
            i = bank_ctr.get(group, 0)
            bank_ctr[group] = i + 1
            return base + (i % (4 if group == "dd" else 2))

        def kps(b):
            return ("ps", b)

        xtm_keys = [("xtm", s, dc) for s in range(4) for dc in range(4)]
        xT_keys = [("xT", kc) for kc in range(KC)]

        def kr2(g):
            return ("r2", g)

        def mmg(out, pairs, reads, writes):
            def fn(e, out=out, pairs=pairs):
                n = len(pairs)
                inst = None
                for i, (l, r) in enumerate(pairs):
                    inst = e.matmul(out, l, r, start=(i == 0), stop=(i == n - 1))
                return inst
            sc.add("pe", fn, reads, writes)

        def act(out, in_, func, reads, writes, bias=None, scale=None, accum_out=None):
            def fn(e):
                kw = {}
                if bias is not None:
                    kw["bias"] = bias
                if scale is not None:
                    kw["scale"] = scale
                if accum_out is not None:
                    kw["accum_out"] = accum_out
                return e.activation(out, in_, func, **kw)
            sc.add("act", fn, reads, writes)

        def vop(eng, name, reads, writes, *a, **kw):
            def fn(e):
                return getattr(e, name)(*a, **kw)
            sc.add(eng, fn, reads, writes)

        ring_ctr = [0]

        def ring_load(dmas, reads):
            s = ring_ctr[0] % NSLOT
            ring_ctr[0] += 1
            slot = wr[s]

            def fn(e, dmas=dmas, slot=slot):
                return [e.dma_start(out=mk(slot), in_=src) for mk, src in dmas]
            sc.add("sp", fn, reads=reads, writes=[("w", s)], dma=f"w{s}", ninc=len(dmas))
            return s, slot

        def const_fn(e):
            return [e.dma_start(out=cst[:], in_=cst_in[:, :]),
                    e.dma_start(out=Lb[:, 0:4, 0:512], in_=wbd_in.rearrange("p (a b) -> p a b", b=512))]
        sc.add("sp", const_fn, reads=[], writes=["cst"] + [kL(j) for j in range(4)], dma="cst", ninc=2)

        cv_ctr = [0]

        def wkeys(name):
            return [("wb", name, r0) for r0 in range(0, wshapes[name][0], 256)]

        def conv_weight(name):
            src, dst = w32[name], wb[name]
            rows, cols = wshapes[name]
            step = 256
            for r0 in range(0, rows, step):
                r1 = min(rows, r0 + step)
                ch = cv_ctr[0] % 4
                cv_ctr[0] += 1

                def fn(e, src=src, dst=dst, r0=r0, r1=r1):
                    return [e.dma_start(out=dst[r0:r1, :], in_=src[r0:r1, :], max_dma_last_dim=2048)]
                sc.add("pool", fn, reads=[], writes=[("wb", name, r0), ("cvchain", ch)], dma=f"cv{ch}", ninc=1)

        for name in ("w1g", "w1u", "w1d", "win", "wkv", "wq", "wout", "w2g", "w2u", "w2d", "wpg", "wpp"):
            conv_weight(name)

        vop("dve", "memset", [], ["onesb"], onesb[:], 1.0)
        vop("dve", "tensor_copy", ["cst"], ["identb"], identb[:], ident32)
        vop("dve", "tensor_copy", [kL(0), kL(1)], ["wabd"], wabd[:].rearrange("p (a c) f -> p a (c f)", a=2),
            Lb[:, 0:2, 0:512])
        vop("dve", "tensor_copy", [kL(2), kL(3)], ["wxbd"], wxbd[:].rearrange("p (a c) f -> p a (c f)", a=2),
            Lb[:, 2:4, 0:512])
        vop("dve", "memset", [], ["halo"], halo[:], 0.0)
        vop("dve", "memset", [], ["carry"], carry[:], 0.0)
        act(clt[:, 0, :], cv[:, :, 7], AF.Exp, ["cst"], ["clt0"], scale=-1.0)
        act(clt[:, 0, :], clt[:, 0, :], AF.Ln, ["clt0"], ["clt0"], bias=1.0)
        vop("dve", "tensor_scalar", ["clt0"], ["cl"], clt[:, 1, :], clt[:, 0, :], -8.0, None, ALU.mult)
        vop("dve", "tensor_scalar", ["clt0"], ["cl"], clt[:, 2, :], clt[:, 0, :], -16.0, None, ALU.mult)

        def load_x(ti):
            src = xs[ti * T:(ti + 1) * T, :].rearrange("(s p) d -> p s d", p=128)

            def fn(e, src=src):
                return [e.dma_start(out=xtm[:], in_=src)]
            sc.add("sp", fn, reads=[], writes=xtm_keys, dma="x", ninc=1)

        evac_ctr = [0]

        def evac_copy(out, in_, reads, writes):
            evac_ctr[0] += 1
            if evac_ctr[0] % 2:
                act(out, in_, AF.Copy, reads, writes)
            else:
                vop("dve", "tensor_copy", reads, writes, out, in_)

        def transposes():
            for kc in range(KC):
                b = bank("m")
                pt = ps[b]

                def fn(e, kc=kc, pt=pt):
                    inst = None
                    for s in range(4):
                        inst = e.transpose(pt[:, s * 128:(s + 1) * 128], xtm[:, s, kc * 128:(kc + 1) * 128], ident32)
                    return inst
                sc.add("pe", fn, reads=[("xtm", s, kc // 4) for s in range(4)] + ["cst"], writes=[kps(b)])
                evac_copy(xT[:, kc, :], pt[:], [kps(b)], [("xT", kc)])

        def prescale():
            for s in range(4):
                act(xtm[:, s, :], xtm[:, s, :], AF.Copy, [("xtm", s, dc) for dc in range(4)], [("xtm", s, dc) for dc in range(4)], scale=ALPHA)

        def layernorm(n):
            s_, slot = ring_load([(lambda sl: sl[:].bitcast(F32)[:, 0:4096].rearrange("p (o n) -> p o n", o=1),
                                   lnp[n:n + 1, :].partition_broadcast(128))], reads=[])
            gb = slot[:].bitcast(F32)
            for s in range(4):
                keys = [("xtm", s, dc) for dc in range(4)]
                c0 = 16 + 8 * s
                st = ("lnst", s)
                for dc in range(4):
                    vop("dve", "bn_stats", keys[dc:dc + 1], [("bnst", s, dc)], bnst4[:, s, dc * 6:(dc + 1) * 6], xtm[:, s, dc * 512:(dc + 1) * 512])
                vop("dve", "bn_aggr", [("bnst", s, dc) for dc in range(4)], [st], sm[:, c0:c0 + 2], bnst4[:, s, :])
                vop("dve", "tensor_scalar", [st], [st], sm[:, c0 + 2:c0 + 3], sm[:, c0 + 1:c0 + 2], LN_EPS, None, ALU.add)
                act(sm[:, c0 + 3:c0 + 4], sm[:, c0 + 2:c0 + 3], AF.Sqrt, [st], [st])
                vop("dve", "reciprocal", [st], [st], sm[:, c0 + 4:c0 + 5], sm[:, c0 + 3:c0 + 4])
                vop("dve", "tensor_scalar", [st], [st], sm[:, c0 + 5:c0 + 6], sm[:, c0:c0 + 1], sm[:, c0 + 4:c0 + 5], -1.0, ALU.mult, ALU.mult)
                act(xtm[:, s, :], xtm[:, s, :], AF.Identity, keys + [st], keys, bias=sm[:, c0 + 5:c0 + 6], scale=sm[:, c0 + 4:c0 + 5])
                vop("dve", "tensor_tensor", keys + [("w", s_)], keys, xtm[:, s, :], xtm[:, s, :], gb[:, 0:2048], ALU.mult)
                vop("pool", "tensor_tensor", keys + [("w", s_)], keys, xtm[:, s, :], xtm[:, s, :], gb[:, 2048:4096], ALU.add)

        def ffn(ng, nu, nd):
            wg, wu, wd = wb[ng], wb[nu], wb[nd]
            pend = {}

            def gu(fg):
                sA, wA = ring_load([(lambda sl: sl[:].rearrange("p (k f) -> p k f", f=512),
                                     wg[:, fg * 512:(fg + 1) * 512].rearrange("(k p) f -> p k f", p=128))],
                                   reads=wkeys(ng))
                sB, wB = ring_load([(lambda sl: sl[:].rearrange("p (k f) -> p k f", f=512),
                                     wu[:, fg * 512:(fg + 1) * 512].rearrange("(k p) f -> p k f", p=128))],
                                   reads=wkeys(nu))
                wAv = wA[:].rearrange("p (k f) -> p k f", f=512)
                wBv = wB[:].rearrange("p (k f) -> p k f", f=512)
                for j in range(4):
                    bgk, buk = bank("g"), bank("u")
                    mmg(ps[bgk][:], [(wAv[:, kc, j * 128:(j + 1) * 128], xT[:, kc, :]) for kc in range(KC)],
                        [("w", sA)] + xT_keys, [kps(bgk)])
                    mmg(ps[buk][:], [(wBv[:, kc, j * 128:(j + 1) * 128], xT[:, kc, :]) for kc in range(KC)],
                        [("w", sB)] + xT_keys, [kps(buk)])
                    lj = (fg * 4 + j) % 2
                    act(L(lj), ps[bgk][:], AF.Silu, [kps(bgk)], [kL(lj)])
                    g = (fg % 2) * 4 + j
                    vop("dve", "tensor_tensor", [kL(lj), kps(buk)], [kr2(g)], r2[:, g, :], L(lj), ps[buk][:], ALU.mult)

            def down(fg):
                sD, wD = ring_load([(lambda sl: sl[:].rearrange("p (j d) -> p j d", d=2048),
                                     wd[fg * 512:(fg + 1) * 512, :].rearrange("(j p) d -> p j d", p=128))],
                                   reads=wkeys(nd))
                wDv = wD[:].rearrange("p (j d) -> p j d", d=2048)
                gr = [(fg % 2) * 4 + j for j in range(4)]
                for s in range(4):
                    for dc in range(4):
                        b = bank("dd")
                        mmg(ps[b][:], [(r2[:, gr[j], s * 128:(s + 1) * 128], wDv[:, j, dc * 512:(dc + 1) * 512])
                                       for j in range(4)],
                            [("w", sD)] + [kr2(g) for g in gr], [kps(b)])
                        vop("dve", "scalar_tensor_tensor", [kps(b), ("xtm", s, dc)], [("xtm", s, dc)],
                            xtm[:, s, dc * 512:(dc + 1) * 512], ps[b][:], 0.5,
                            xtm[:, s, dc * 512:(dc + 1) * 512], ALU.mult, ALU.add)

            gu(0)
            for fg in range(NFG):
                if fg + 1 < NFG:
                    gu(fg + 1)
                down(fg)

        def rms_feat(c0, gcol, src_slot_key, wslot_view, outL):
            bss = bank("m")
            for c in range(4):
                b = bank("g")
                mmg(ps[b][:], [(wslot_view[:, kc, c * 128:(c + 1) * 128], xT[:, kc, :]) for kc in range(KC)],
                    [src_slot_key] + xT_keys, [kps(b)])
                vop("dve", "tensor_copy", [kps(b)], [kL(4 + c)], L(4 + c), ps[b][:])
                if stop == "kv3a1":
                    continue
                vop("dve", "tensor_tensor", [kL(4 + c)], [kL(8)], L(8), L(4 + c), L(4 + c), ALU.mult)
                if stop == "kv3a2":
                    continue
                hi = Lb[:, 9, :].bitcast(BF16)[:, 0:512]
                lo = Lb[:, 9, :].bitcast(BF16)[:, 512:1024]
                vop("dve", "tensor_copy", [kL(8)], [kL(9)], hi, L(8))
                vop("dve", "tensor_tensor", [kL(8), kL(9)], [kL(9)], lo, L(8), hi, ALU.subtract)
                if stop == "kv3a3":
                    continue

                def fn(e, c=c, bss=bss, hi=hi, lo=lo):
                    e.matmul(ps[bss][:], onesb[:], hi, start=(c == 0), stop=False)
                    return e.matmul(ps[bss][:], onesb[:], lo, start=False, stop=(c == 3))
                sc.add("pe", fn, reads=["onesb", kL(9)], writes=[kps(bss)])
            if stop in ("kv3a1", "kv3a2", "kv3a3", "kv3a4"):
                return
            vop("dve", "tensor_scalar", [kps(bss)], [kL(10)], L(10), ps[bss][:], 1.0 / 512.0, RMS_EPS, ALU.mult, ALU.add)
            act(L(10), L(10), AF.Sqrt, [kL(10)], [kL(10)])
            vop("dve", "reciprocal", [kL(10)], [kL(11)], L(11), L(10))
            for c in range(4):
                vop("dve", "scalar_tensor_tensor", [kL(4 + c), kL(11), "cst"], [kL(outL + c // 2)],
                    Lb[:, outL + c // 2, :].bitcast(BF16)[:, (c % 2) * 512:(c % 2) * 512 + 512], L(4 + c),
                    cst[:, gcol + c:gcol + c + 1], L(11), ALU.mult, ALU.mult)

        def nrm(outL, c):
            return Lb[:, outL + c // 2, :].bitcast(BF16)[:, (c % 2) * 512:(c % 2) * 512 + 512]

        def rope_tables(ti):
            def fn(e):
                return [e.dma_start(out=L(2, T, I32, 64).rearrange("p (o n) -> p o n", o=1),
                                    in_=pos[0:1, ti * T:(ti + 1) * T].partition_broadcast(64))]
            sc.add("sp", fn, reads=[], writes=[kL(2)], dma="pos", ninc=1)
            A = lambda j: L(j, T, F32, 64)
            vop("dve", "tensor_copy", [kL(2)], [kL(3)], A(3), L(2, T, I32, 64))
            vop("dve", "tensor_scalar", [kL(3), "cst"], [kL(3)], A(3), A(3), invf_ap, None, ALU.mult)
            vop("dve", "tensor_scalar", [kL(3)], [kL(0)], A(0), A(3), 1.0 / (2 * PI), None, ALU.mult)
            vop("dve", "tensor_copy", [kL(0)], [kL(2)], L(2, T, I32, 64), A(0))
            vop("dve", "tensor_copy", [kL(2)], [kL(0)], A(0), L(2, T, I32, 64))
            C1 = 6.28125
            C2_ = float(2 * np.pi - 6.28125)
            vop("dve", "scalar_tensor_tensor", [kL(0), kL(3)], [kL(3)], A(3), A(0), -C1, A(3), ALU.mult, ALU.add)
            vop("dve", "scalar_tensor_tensor", [kL(0), kL(3)], [kL(3)], A(3), A(0), -C2_, A(3), ALU.mult, ALU.add)
            vop("dve", "tensor_scalar", [kL(3)], [kL(0)], A(0), A(3), PI, -2 * PI, ALU.is_gt, ALU.mult)
            vop("dve", "tensor_tensor", [kL(0), kL(3)], [kL(3)], A(3), A(3), A(0), ALU.add)
            vop("dve", "tensor_scalar", [kL(3)], [kL(0)], A(0), A(3), -PI, 2 * PI, ALU.is_lt, ALU.mult)
            vop("dve", "tensor_tensor", [kL(0), kL(3)], [kL(3)], A(3), A(3), A(0), ALU.add)
            vop("dve", "tensor_scalar", [kL(3)], [kL(3)], A(3), A(3), PI, -PI, ALU.min, ALU.max)
            act(A(1), A(3), AF.Sin, [kL(3)], [kL(1)])
            vop("dve", "tensor_scalar", [kL(1), "cst"], [kL(1)], A(1), A(1), sgn_ap, None, ALU.mult)
            vop("dve", "scalar_tensor_tensor", [kL(3)], [kL(3)], A(3), A(3), -1.0, A(3), ALU.mult, ALU.max)
            vop("dve", "tensor_scalar", [kL(3)], [kL(3)], A(3), A(3), -1.0, PI / 2, ALU.mult, ALU.add)
            act(A(0), A(3), AF.Sin, [kL(3)], [kL(0)])

        def rope_apply(pa, pb, out_ap, scale, reads, writes):
            vop("dve", "tensor_tensor", [kps(pa), kL(0)], [kL(12)], L(12, T, F32, 64), ps[pa][0:64, :], L(0, T, F32, 64), ALU.mult)
            vop("dve", "tensor_tensor", [kps(pb), kL(1)], [kL(13)], L(13, T, F32, 64), ps[pb][0:64, :], L(1, T, F32, 64), ALU.mult)
            vop("dve", "scalar_tensor_tensor", [kL(12), kL(13)] + reads, writes, out_ap, L(12, T, F32, 64), scale,
                L(13, T, F32, 64), ALU.mult, ALU.add)

        def kv_path(ti):
            sKV, wKV = ring_load([(lambda sl: sl[:].rearrange("p (k f) -> p k f", f=512),
                                   wb["win"][:, 2560:3072].rearrange("(k p) f -> p k f", p=128))], reads=wkeys("win"))
            rms_feat(0, C_KVG, ("w", sKV), wKV[:].rearrange("p (k f) -> p k f", f=512), 12)
            sW, wW = ring_load([(lambda sl: sl[:].rearrange("p (k f) -> p k f", f=2048),
                                 wb["wkv"].rearrange("(k p) f -> p k f", p=128))], reads=wkeys("wkv"))
            wv = wW[:].rearrange("p (k f) -> p k f", f=2048)
            nkeys = [kL(12), kL(13)]
            if stop and stop.startswith("kv3a"):
                return
            for h in range(8):
                b = bank("u")
                mmg(ps[b][:], [(wv[:, kc, h * 128:(h + 1) * 128], nrm(12, kc)) for kc in range(4)],
                    [("w", sW)] + nkeys, [kps(b)])
                evac_copy(r2[:, h, :], ps[b][:], [kps(b)], [kr2(h)])

            def fnk(e):
                return [e.dma_start(out=KTs[:, :, ti * T:(ti + 1) * T].rearrange("h d t -> d h t"), in_=r2[:, 0:8, :])]
            sc.add("act", fnk, reads=[kr2(h) for h in range(8)], writes=[("KT", ti)], dma="stK", ninc=1)
            if stop == "kv3b":
                return
            vst = r2[:, 8:16, :].rearrange("p (s a) t -> p s (a t)", s=4)
            for s in range(4):
                for hf in range(2):
                    b = bank("d")
                    mmg(ps[b][:], [(nrm(12, kc)[:, s * 128:(s + 1) * 128], wv[:, kc, 1024 + hf * 512:1024 + (hf + 1) * 512])
                                   for kc in range(4)], [("w", sW)] + nkeys, [kps(b)])
                    evac_copy(vst[:, s, hf * 512:(hf + 1) * 512], ps[b][:], [kps(b)], [kr2(8 + 2 * s + hf)])

            def fnv(e):
                return [e.dma_start(out=Vs[:, :, ti * 4 + s, :].rearrange("h p d -> p h d"),
                                    in_=vst[:, s, :].rearrange("p (h d) -> p h d", d=128)) for s in range(4)]
            sc.add("act", fnv, reads=[kr2(g) for g in range(8, 16)], writes=[("V", ti)], dma="stV", ninc=4)

        def kpe_path(ti):
            sP, wP = ring_load([(lambda sl: sl[:, 0:2048].rearrange("p (k f) -> p k f", f=128),
                                 wb["win"][:, 3072:3200].rearrange("(k p) f -> p k f", p=128))], reads=wkeys("win"))
            wpv = wP[:, 0:2048].rearrange("p (k f) -> p k f", f=128)
            pa, pb = bank("g"), bank("u")
            mmg(ps[pa][0:64, :], [(wpv[:, kc, 0:64], xT[:, kc, :]) for kc in range(KC)], [("w", sP)] + xT_keys, [kps(pa)])
            mmg(ps[pb][0:64, :], [(wpv[:, kc, 64:128], xT[:, kc, :]) for kc in range(KC)], [("w", sP)] + xT_keys, [kps(pb)])
            rope_apply(pa, pb, L(11, T, BF16, 64), 1.0, [], [kL(11)])

            def fnp(e):
                return [e.dma_start(out=KPEs[:, ti * T:(ti + 1) * T], in_=L(11, T, BF16, 64))]
            sc.add("act", fnp, reads=[kL(11)], writes=[("KPE", ti)], dma="stP", ninc=1)

        def q_path():
            sQ, wQ = ring_load([(lambda sl: sl[:].rearrange("p (k f) -> p k f", f=512),
                                 wb["win"][:, 2048:2560].rearrange("(k p) f -> p k f", p=128))], reads=wkeys("win"))
            rms_feat(0, C_QG, ("w", sQ), wQ[:].rearrange("p (k f) -> p k f", f=512), 12)
            sW, wW = ring_load([(lambda sl: sl[:].rearrange("p (k f) -> p k f", f=2048),
                                 wb["wq"].rearrange("(k p) f -> p k f", p=128))], reads=wkeys("wq"))
            wv = wW[:].rearrange("p (k f) -> p k f", f=2048)
            nkeys = [kL(12), kL(13)]
            for h in range(8):
                b = bank("d")
                mmg(ps[b][:], [(wv[:, kc, h * 256:h * 256 + 128], nrm(12, kc)) for kc in range(4)],
                    [("w", sW)] + nkeys, [kps(b)])
                act(r2[:, 16 + h, :], ps[b][:], AF.Copy, [kps(b)], [kr2(16 + h)], scale=QSCALE)
                pa, pb = bank("g"), bank("u")
                mmg(ps[pa][0:64, :], [(wv[:, kc, h * 256 + 128:h * 256 + 192], nrm(12, kc)) for kc in range(4)],
                    [("w", sW)] + nkeys, [kps(pa)])
                mmg(ps[pb][0:64, :], [(wv[:, kc, h * 256 + 192:h * 256 + 256], nrm(12, kc)) for kc in range(4)],
                    [("w", sW)] + nkeys, [kps(pb)])
                vop("dve", "tensor_tensor", [kps(pa), kL(0)], [kL(8)], L(8, T, F32, 64), ps[pa][0:64, :], L(0, T, F32, 64), ALU.mult)
                vop("dve", "tensor_tensor", [kps(pb), kL(1)], [kL(9)], L(9, T, F32, 64), ps[pb][0:64, :], L(1, T, F32, 64), ALU.mult)
                vop("dve", "tensor_tensor", [kL(8), kL(9)], [kL(8)], L(8, T, F32, 64), L(8, T, F32, 64), L(9, T, F32, 64), ALU.add)
                vop("dve", "tensor_scalar", [kL(8)], [kr2(24 + h)], r2[0:64, 24 + h, :], L(8, T, F32, 64), QSCALE, None, ALU.mult)

        def lru(ti, own):
            first_own = (ti == nth)
            slots = {}
            for half in range(2):
                sA, wA = ring_load([(lambda sl: sl[:].rearrange("p (k f) -> p k f", f=512),
                                     wb["win"][:, half * 512:(half + 1) * 512].rearrange("(k p) f -> p k f", p=128))],
                                   reads=wkeys("win"))
                slots[("a", half)] = (sA, wA[:].rearrange("p (k f) -> p k f", f=512))
                if own:
                    sG, wG = ring_load([(lambda sl: sl[:].rearrange("p (k f) -> p k f", f=512),
                                         wb["win"][:, 1024 + half * 512:1024 + (half + 1) * 512].rearrange("(k p) f -> p k f", p=128))],
                                       reads=wkeys("win"))
                    slots[("g", half)] = (sG, wG[:].rearrange("p (k f) -> p k f", f=512))
                for cc in range(4):
                    c = half * 4 + cc
                    sA, wAv = slots[("a", half)]
                    b = bank("g")
                    mmg(ps[b][:], [(wAv[:, kc, cc * 128:(cc + 1) * 128], xT[:, kc, :]) for kc in range(KC)],
                        [("w", sA)] + xT_keys, [kps(b)])
                    lin = Lb[:, 2, 0:515]
                    if first_own:
                        vop("dve", "tensor_scalar", ["halo", "cst"], [kL(2)], lin[:, 0:3], halo[:, c, 0:3], flag_ap, None, ALU.mult)
                        vop("dve", "tensor_scalar", ["carry", "cst"], ["carry"], carry[:, c:c + 1], carry[:, c:c + 1], flag_ap, None, ALU.mult)
                    else:
                        vop("dve", "tensor_copy", ["halo"], [kL(2)], lin[:, 0:3], halo[:, c, 0:3])
                    act(lin[:, 3:515], ps[b][:], AF.Copy, [kps(b)], [kL(2)])
                    vop("dve", "tensor_copy", [kL(2)], ["halo"], halo[:, c, 0:3], lin[:, 512:515])
                    act(L(3), lin[:, 3:515], AF.Identity, [kL(2), "cst"], [kL(3)], bias=cv[:, c, 4:5], scale=cv[:, c, 3:4])
                    for k in range(3):
                        vop("dve", "scalar_tensor_tensor", [kL(2), kL(3), "cst"], [kL(3)], L(3), lin[:, k:k + 512],
                            cv[:, c, k:k + 1], L(3), ALU.mult, ALU.add)
                    act(L(4, T, BF16), L(3), AF.Copy, [kL(3)], [kL(4)])
                    ba, bx = bank("u"), bank("d")
                    mmg(ps[ba][:], [(wabd[:, c, :], L(4, T, BF16))], ["wabd", kL(4)], [kps(ba)])
                    mmg(ps[bx][:], [(wxbd[:, c, :], L(4, T, BF16))], ["wxbd", kL(4)], [kps(bx)])
                    act(L(5), ps[ba][:], AF.Sigmoid, [kps(ba), "cst"], [kL(5)], bias=cv[:, c, 5:6])
                    act(L(6), ps[bx][:], AF.Sigmoid, [kps(bx), "cst"], [kL(6)], bias=cv[:, c, 6:7])
                    act(L(7), L(5), AF.Exp, [kL(5), "cl"], [kL(7)], scale=clt[:, 1, c:c + 1])
                    act(L(5), L(5), AF.Exp, [kL(5), "cl"], [kL(5)], scale=clt[:, 2, c:c + 1])
                    act(L(5), L(5), AF.Sqrt, [kL(5)], [kL(5)], bias=1.0, scale=-1.0)
                    vop("dve", "tensor_tensor", [kL(6), kL(3)], [kL(6)], L(6), L(6), L(3), ALU.mult)
                    vop("dve", "tensor_tensor", [kL(6), kL(5)], [kL(6)], L(6), L(6), L(5), ALU.mult)
                    vop("dve", "tensor_tensor_scan", [kL(7), kL(6), "carry"], [kL(8)], L(8), L(7), L(6),
                        carry[:, c:c + 1], ALU.mult, ALU.add)
                    vop("dve", "tensor_copy", [kL(8)], ["carry"], carry[:, c:c + 1], L(8)[:, T - 1:T])
                    if own:
                        sG, wGv = slots[("g", half)]
                        bq = bank("m")
                        mmg(ps[bq][:], [(wGv[:, kc, cc * 128:(cc + 1) * 128], xT[:, kc, :]) for kc in range(KC)],
                            [("w", sG)] + xT_keys, [kps(bq)])
                        act(L(9), ps[bq][:], AF.Copy, [kps(bq)], [kL(9)])
                        vop("dve", "tensor_tensor", [kL(9)], [kL(10)], L(10), L(9), L(9), ALU.mult)
                        vop("dve", "tensor_scalar", [kL(10)], [kL(10)], L(10), L(10), 0.044715, 1.0, ALU.mult, ALU.add)
                        vop("dve", "tensor_tensor", [kL(10), kL(9)], [kL(10)], L(10), L(10), L(9), ALU.mult)
                        act(L(10), L(10), AF.Sigmoid, [kL(10)], [kL(10)], scale=1.5957691216057308)
                        vop("dve", "tensor_tensor", [kL(10), kL(9)], [kL(10)], L(10), L(10), L(9), ALU.mult)
                        vop("dve", "tensor_tensor", [kL(10), kL(8)], [kr2(c)], r2[:, c, :], L(10), L(8), ALU.mult)

        def out_proj():
            for dc in range(4):
                sW, wW = ring_load([(lambda sl: sl[:].rearrange("p (k f) -> p k f", f=512),
                                     wb["wout"][:, dc * 512:(dc + 1) * 512].rearrange("(k p) f -> p k f", p=128))],
                                   reads=wkeys("wout"))
                wv = wW[:].rearrange("p (k f) -> p k f", f=512)
                for s in range(4):
                    b = bank("d")
                    mmg(ps[b][:], [(r2[:, kc, s * 128:(s + 1) * 128], wv[:, kc, :]) for kc in range(KC)],
                        [("w", sW)] + [kr2(g) for g in range(16)], [kps(b)])
                    vop("dve", "scalar_tensor_tensor", [kps(b), ("xtm", s, dc)], [("xtm", s, dc)],
                        xtm[:, s, dc * 512:(dc + 1) * 512], xtm[:, s, dc * 512:(dc + 1) * 512], ALPHA, ps[b][:],
                        ALU.mult, ALU.add)

        def ple(to):
            def fnp(e):
                return [e.dma_start(out=Lb[:, a, 0:512].rearrange("p (s c) -> p s c", c=256),
                                    in_=ps_in[to * T + a * 256:to * T + (a + 1) * 256, :].rearrange("(s p) c -> p s c", p=128))
                        for a in range(2)]
            sc.add("sp", fnp, reads=[], writes=[kL(0), kL(1)], dma="p", ninc=2)

            class _PV:
                def __getitem__(self, idx):
                    _, s, cs = idx
                    return Lb[:, s // 2, (s % 2) * 256 + cs.start:(s % 2) * 256 + cs.stop]
            ptv = _PV()
            pT = Lb[:, 2, :].bitcast(BF16)[:, 0:1024].rearrange("p (k t) -> p k t", t=512)
            for k2 in range(2):
                b = bank("m")

                def fn(e, k2=k2, b=b):
                    inst = None
                    for s in range(4):
                        inst = e.transpose(ps[b][:, s * 128:(s + 1) * 128], ptv[:, s, k2 * 128:(k2 + 1) * 128], ident32)
                    return inst
                sc.add("pe", fn, reads=[kL(0), kL(1), "cst"], writes=[kps(b)])
                vop("dve", "tensor_copy", [kps(b)], [kL(2)], pT[:, k2, :], ps[b][:])
            if stop == "ple1":
                return
            sB, wB_ = ring_load([(lambda sl: sl[:, 0:4096].rearrange("p (k f) -> p k f", f=2048),
                                  wb["wpp"].rearrange("(k p) f -> p k f", p=128)),
                                 (lambda sl: sl[:].bitcast(F32)[:, 2048:4096].rearrange("p (o n) -> p o n", o=1),
                                  bg_in[0:1, :].partition_broadcast(128))], reads=wkeys("wpp"))
            wpp = wB_[:, 0:4096].rearrange("p (k f) -> p k f", f=2048)
            bgB = wB_[:].bitcast(F32)[:, 2048:4096]
            if stop == "ple2":
                vop("dve", "tensor_copy", [("w", sB)], [kL(3)], L(3), bgB[:, 0:512])
                return
            for dc in range(1 if stop == "ple3" else 4):
                sW, wW = ring_load([(lambda sl: sl[:].rearrange("p (k f) -> p k f", f=512),
                                     wb["wpg"][:, dc * 512:(dc + 1) * 512].rearrange("(k p) f -> p k f", p=128))],
                                   reads=wkeys("wpg"))
                wv = wW[:].rearrange("p (k f) -> p k f", f=512)
                for s in range(4):
                    bgk, bpk = bank("g"), bank("u")
                    mmg(ps[bgk][:], [(xT[:, kc, s * 128:(s + 1) * 128], wv[:, kc, :]) for kc in range(KC)],
                        [("w", sW)] + xT_keys, [kps(bgk)])
                    mmg(ps[bpk][:], [(pT[:, k2, s * 128:(s + 1) * 128], wpp[:, k2, dc * 512:(dc + 1) * 512]) for k2 in range(2)],
                        [("w", sB), kL(2)], [kps(bpk)])
                    lj = 3 + (s % 2)
                    vop("dve", "tensor_tensor", [kps(bgk), ("w", sB)], [kL(lj)], L(lj), ps[bgk][:], bgB[:, dc * 512:(dc + 1) * 512], ALU.add)
                    act(L(lj), L(lj), AF.Sigmoid, [kL(lj)], [kL(lj)])
                    vop("dve", "tensor_tensor", [kL(lj), kps(bpk)], [kL(lj)], L(lj), L(lj), ps[bpk][:], ALU.mult)
                    vop("dve", "scalar_tensor_tensor", [kL(lj), ("xtm", s, dc)], [("xtm", s, dc)],
                        xtm[:, s, dc * 512:(dc + 1) * 512], xtm[:, s, dc * 512:(dc + 1) * 512], ALPHA, L(lj),
                        ALU.mult, ALU.add)

        def store_out(to):
            def fn(e):
                return [e.dma_start(out=yout[to * T:(to + 1) * T, :].rearrange("(s p) d -> p s d", p=128), in_=xtm[:])]
            sc.add("act", fn, reads=xtm_keys, writes=[("y", to)], dma="sty", ninc=1)

        def attention_full(ti):
            nkc = ti + 1
            ystage = [Lb[:, 4, :].bitcast(BF16)[:, 0:1024], Lb[:, 5, :].bitcast(BF16)[:, 0:1024]]
            for h in range(8):
                nk = nkc * T
                sK, wKt = ring_load([(lambda sl, nk=nk: sl[:, 0:nk], KTs[h, :, 0:nk])], reads=[("KT", j) for j in range(nkc)])
                sP, wPt = ring_load([(lambda sl, nk=nk: sl[0:64, 0:nk], KPEs[:, 0:nk])], reads=[("KPE", j) for j in range(nkc)])
                sV, wVt = ring_load([(lambda sl, nkc=nkc: sl[:, 0:nkc * 512].rearrange("p (b d) -> p b d", d=128),
                                      Vs[h, :, 0:nkc * 4, :])], reads=[("V", j) for j in range(nkc)])
                wK, wP = wKt, wPt
                wVv = wVt[:].rearrange("p (b d) -> p b d", d=128)
                for qb in range(4):
                    qs = slice(qb * 128, (qb + 1) * 128)
                    qn = r2[:, 16 + h, qs]
                    qp = r2[0:64, 24 + h, qs]
                    ndiag = (qb + 1) * 128

                    def scores(b, kcx, n, qn=qn, qp=qp, wK=wK, wP=wP, sK=sK, sP=sP, h=h):
                        mmg(ps[b][:, 0:n], [(qn, wK[:, kcx * T:kcx * T + n]), (qp, wP[0:64, kcx * T:kcx * T + n])],
                            [("w", sK), ("w", sP), kr2(16 + h), kr2(24 + h)], [kps(b)])
                    for kcx in range(nkc):
                        diag = (kcx == ti)
                        n = ndiag if diag else T
                        b = bank("g")
                        scores(b, kcx, n)
                        if diag:
                            vop("dve", "tensor_tensor", [kps(b), "cst"], [kL(6)], L(6)[:, 0:n], ps[b][:, 0:n],
                                cm3[:, (3 - qb) * 128:(3 - qb) * 128 + n], ALU.add)
                            vop("dve", "reduce_max", [kL(6)], ["mxt"], mxt[:, kcx:kcx + 1], L(6)[:, 0:n], AX.X)
                        else:
                            vop("dve", "reduce_max", [kps(b)], ["mxt"], mxt[:, kcx:kcx + 1], ps[b][:, 0:n], AX.X)
                    vop("dve", "tensor_scalar", ["mxt", "cst"], ["mxt"], mxt[:, 0:nth], mxt[:, 0:nth], mbias_ap, None, ALU.add)
                    vop("dve", "reduce_max", ["mxt"], ["mxm"], sm[:, 8:9], mxt[:, 0:nkc], AX.X)
                    vop("dve", "tensor_scalar", ["mxm"], ["negm"], sm[:, 9:10], sm[:, 8:9], -1.0, None, ALU.mult)
                    vop("dve", "tensor_scalar", ["negm", "cst"], ["negmo"], sm[:, 10:11], sm[:, 9:10], mbias_ap, None, ALU.add)
                    bo = bank("d")
                    nblk_tot = (nkc - 1) * 4 + (qb + 1)

                    def stA(kcx, qb=qb, ndiag=ndiag, scores=scores):
                        diag = (kcx == ti)
                        n = ndiag if diag else T
                        pj = 7 + (kcx % 2)
                        Pt = L(pj, T, BF16)
                        rs = mxt[:, 20 + kcx:21 + kcx]
                        if diag:
                            act(Pt[:, 0:n], L(6)[:, 0:n], AF.Exp, [kL(6), "negm"], [kL(pj), ("rs", kcx)], bias=sm[:, 9:10], accum_out=rs)
                        else:
                            b = bank("g")
                            scores(b, kcx, n)
                            act(Pt[:, 0:n], ps[b][:, 0:n], AF.Exp, [kps(b), "negm", "negmo"], [kL(pj), ("rs", kcx)],
                                bias=(sm[:, 10:11] if kcx < nth else sm[:, 9:10]), accum_out=rs)

                    def stB(kcx, ndiag=ndiag):
                        n = ndiag if kcx == ti else T
                        pj = 7 + (kcx % 2)
                        Pt = L(pj, T, BF16)
                        nb = n // 128
                        bt = bank("u")
                        ptp = ps[bt][:].bitcast(BF16)

                        def fnt(e, Pt=Pt, nb=nb, ptp=ptp):
                            inst = None
                            for jb in range(nb):
                                inst = e.transpose(ptp[:, jb * 128:(jb + 1) * 128], Pt[:, jb * 128:(jb + 1) * 128], identb[:])
                            return inst
                        sc.add("pe", fnt, reads=[kL(pj), "identb"], writes=[kps(bt)])
                        tj = 9 + (kcx % 2)
                        evac_copy(L(tj, T, BF16)[:, 0:n], ptp[:, 0:n], [kps(bt)], [kL(tj)])

                    def stC(kcx, ndiag=ndiag, bo=bo, nblk_tot=nblk_tot, wVv=wVv, sV=sV):
                        n = ndiag if kcx == ti else T
                        nb = n // 128
                        tj = 9 + (kcx % 2)
                        PT = L(tj, T, BF16)
                        blk_i = kcx * 4

                        def fnpv(e):
                            inst = None
                            for jb in range(nb):
                                gi = blk_i + jb
                                inst = e.matmul(ps[bo][:, 0:128], PT[:, jb * 128:(jb + 1) * 128], wVv[:, kcx * 4 + jb, :],
                                                start=(gi == 0), stop=(gi == nblk_tot - 1))
                            return inst
                        sc.add("pe", fnpv, reads=[kL(tj), ("w", sV)], writes=[kps(bo)])

                    stA(0)
                    if nkc > 1:
                        stA(1)
                    stB(0)
                    for kcx in range(nkc):
                        if kcx + 2 < nkc:
                            stA(kcx + 2)
                        if kcx + 1 < nkc:
                            stB(kcx + 1)
                        stC(kcx)
                    vop("dve", "reduce_sum", [("rs", k) for k in range(nkc)], ["lsum"], sm[:, 11:12], mxt[:, 20:20 + nkc], AX.X)
                    vop("dve", "reciprocal", ["lsum"], ["linv"], sm[:, 12:13], sm[:, 11:12])
                    yj = 11 + (qb % 2)
                    vop("dve", "tensor_scalar", [kps(bo), "linv"], [kL(yj)], L(yj, 128, BF16), ps[bo][:, 0:128], sm[:, 12:13], None, ALU.mult)
                    bt = bank("u")
                    ptp = ps[bt][:].bitcast(BF16)

                    def fny(e, yj=yj, ptp=ptp):
                        return e.transpose(ptp[:, 0:128], L(yj, 128, BF16), identb[:])
                    sc.add("pe", fny, reads=[kL(yj), "identb"], writes=[kps(bt)])
                    evac_copy(r2[:, 8 + h, qs], ptp[:, 0:128], [kps(bt)], [kr2(8 + h)])

        order = ["x", "tr", "ffn", "ln1", "kv", "lru", "attn", "ln2", "ffn2", "full"]
        lvl = (4 if stop.startswith("kv") else 9 if stop.startswith("ple") else order.index(stop)) if stop else len(order) - 1

        def tile_prog(ti):
            own = ti >= nth
            to = ti - nth
            load_x(ti)
            if lvl < 1:
                return
            transposes()
            prescale()
            if lvl < 2:
                return
            ffn("w1g", "w1u", "w1d")
            if lvl < 3:
                return
            layernorm(0)
            if lvl < 4:
                return
            transposes()
            if stop == "kv0":
                return
            rope_tables(ti)
            if stop == "kv1":
                return
            kpe_path(ti)
            if stop == "kv2":
                return
            kv_path(ti)
            if stop and stop.startswith("kv3"):
                return
            if own:
                q_path()
            if lvl < 5:
                return
            lru(ti, own)
            if not own or lvl < 6:
                return
            attention_full(ti)
            if lvl < 7:
                return
            out_proj()
            layernorm(1)
            if lvl < 8:
                return
            transposes()
            prescale()
            ffn("w2g", "w2u", "w2d")
            layernorm(2)
            if lvl < 9:
                return
            transposes()
            ple(to)
            if stop and stop.startswith("ple"):
                return
            layernorm(3)

        for ti in range(NTILE):
            tile_prog(ti)
            if ti >= nth:
                store_out(ti - nth)
        sc.add("sp", None, reads=[("y", t_) for t_ in range(nth)], writes=[])
        sc.emit(nc, stack)
    return nc


def _host_layouts(inp, S):
    f = lambda a: np.ascontiguousarray(np.asarray(a, dtype=np.float32))
    w = {}
    w["w1g"], w["w1u"], w["w1d"] = f(inp["ffn1_w_gate"][0]), f(inp["ffn1_w_up"][0]), f(inp["ffn1_w_down"][0])
    w["w2g"], w["w2u"], w["w2d"] = f(inp["ffn2_w_gate"][0]), f(inp["ffn2_w_up"][0]), f(inp["ffn2_w_down"][0])
    win = f(inp["w_in"][0])
    kpe = win[:, 3072:3136]
    w["win"] = np.ascontiguousarray(np.concatenate([win, kpe[:, 32:64], kpe[:, 0:32]], axis=1))
    wq = f(inp["w_q_up"][0]).reshape(512, 8, 192)
    w["wq"] = np.ascontiguousarray(np.concatenate([wq, wq[:, :, 160:192], wq[:, :, 128:160]], axis=2).reshape(512, 2048))
    wkv = f(inp["w_kv_up"][0]).reshape(512, 8, 256)
    w["wkv"] = np.ascontiguousarray(np.concatenate([wkv[:, :, 0:128].reshape(512, 1024), wkv[:, :, 128:256].reshape(512, 1024)], axis=1))
    w["wout"], w["wpg"], w["wpp"] = f(inp["w_out"][0]), f(inp["ple_w_gate"][0]), f(inp["ple_w_proj"][0])
    lnp = np.stack([np.concatenate([f(inp[f"ln{i}_g"][0]), f(inp[f"ln{i}_b"][0])]) for i in (1, 2, 3, 4)])
    bg = f(inp["ple_b_gate"][0]).reshape(1, 2048)
    cst = np.zeros((128, NCST), np.float32)
    cvv = np.zeros((128, 8, 8), np.float32)
    chan = lambda v: f(v).reshape(8, 128).T
    cw = f(inp["conv_w"][0])
    for k in range(4):
        cvv[:, :, k] = chan(cw[k])
    cvv[:, :, 4] = chan(inp["conv_b"][0])
    cvv[:, :, 5] = chan(f(inp["lru_b_a"][0]).reshape(-1))
    cvv[:, :, 6] = chan(f(inp["lru_b_x"][0]).reshape(-1))
    cvv[:, :, 7] = chan(inp["lru_lambda"][0])
    cst[:, C_CV:C_CV + 64] = cvv.reshape(128, 64)
    cst[:, C_QG:C_QG + 4] = f(inp["q_norm_g"][0]).reshape(4, 128).T
    cst[:, C_KVG:C_KVG + 4] = f(inp["kv_norm_g"][0]).reshape(4, 128).T
    invf = (10000.0 ** (-np.arange(0, 64, 2, dtype=np.float32) / np.float32(64))).astype(np.float32)
    cst[0:64, C_ROPE] = np.tile(invf, 2)
    cst[0:64, C_ROPE + 1] = np.concatenate([-np.ones(32, np.float32), np.ones(32, np.float32)])
    cst[:, C_ID:C_ID + 128] = np.eye(128, dtype=np.float32)
    qi = np.arange(128)[:, None]
    kj = np.arange(512)[None, :]
    cst[:, C_CM:C_CM + 512] = np.where(kj <= qi + 384, 0.0, NEG).astype(np.float32)
    wbd = np.zeros((128, 2, 8, 128), np.float32)
    wa, wx = f(inp["lru_w_a"][0]), f(inp["lru_w_x"][0])
    for c in range(8):
        for hb in range(2):
            wbd[hb * 64:(hb + 1) * 64, 0, c, hb * 64:(hb + 1) * 64] = wa[2 * c + hb]
            wbd[hb * 64:(hb + 1) * 64, 1, c, hb * 64:(hb + 1) * 64] = wx[2 * c + hb]
    shared = dict(w, lnp=lnp, bg=bg, wbd=wbd.reshape(128, 2048))
    x = f(inp["x"])
    p = f(inp["p"][0])
    posn = np.asarray(inp["positions"]).astype(np.int32)
    S2 = S // 2
    maps = []
    for c in range(8):
        b, h = c // 2, c % 2
        own = slice(h * S2, (h + 1) * S2)
        oth = slice((1 - h) * S2, (2 - h) * S2)
        cc = cst.copy()
        cc[:, C_FLAG] = 1.0 if h == 1 else 0.0
        cc[:, C_FLAG + 1] = 0.0 if h == 1 else NEG
        m = dict(shared)
        m["xs"] = np.ascontiguousarray(np.concatenate([x[b, oth], x[b, own]], axis=0))
        m["ps"] = np.ascontiguousarray(p[b, own])
        m["pos"] = np.ascontiguousarray(np.concatenate([posn[b, oth], posn[b, own]])[None, :])
        m["cst"] = cc
        maps.append(m)
    return maps


_NC_CACHE = {}


def kernel(**inputs):
    x = np.asarray(inputs["x"])
    B, S, _ = x.shape
    nth = S // 2 // T
    inputs = dict(inputs)
    stop = inputs.pop("_stop", None)
    key = (nth, stop)
    if key not in _NC_CACHE:
        _NC_CACHE[key] = build(nth, stop)
    nc = _NC_CACHE[key]
    ncores = inputs.pop("_ncores", 8)
    maps = _host_layouts(inputs, S)[:ncores]
    res = run_bass_kernel_spmd(nc, maps, core_ids=list(range(ncores)))
    out = np.zeros((B, S, D), np.float32)
    S2 = S // 2
    for c in range(ncores):
        b, h = c // 2, c % 2
        out[b, h * S2:(h + 1) * S2] = np.asarray(res.results[c]["y"], dtype=np.float32).reshape(S2, D)
    return out
```
